# Optimizing a Trainium2 kernel written in Bass

```python
import jax, jax.numpy as jnp
from jax import lax
import numpy as np

D_MODEL = 1024
BATCH = 4
SEQ = 8192
DEPTH = 4

N_MIXERS = 2
POOL_WINDOWS = (2, 4, 8, 16)
N_POOL_GROUPS = len(POOL_WINDOWS)
POOL_GROUP_DIM = D_MODEL // N_POOL_GROUPS
CONV_KERNEL = 31
D_FF = ((8 * D_MODEL // 3 + 255) // 256) * 256
RMS_EPS = 1e-6
LN_EPS = 1e-5
FFN_RESIDUAL_WEIGHT = 0.5
N_POOL_LAYERS = len(range(0, DEPTH, N_MIXERS))
N_CONV_LAYERS = len(range(1, DEPTH, N_MIXERS))

kernel_name = "hybrid_pool_conformer_macaron_encoder"


def rmsnorm(x, g):
    xf = x.astype(jnp.float32)
    y = xf * lax.rsqrt(jnp.mean(xf * xf, axis=-1, keepdims=True) + RMS_EPS)
    return (y * g.astype(jnp.float32)).astype(x.dtype)


def layernorm(x, g, b):
    xf = x.astype(jnp.float32)
    mu = jnp.mean(xf, axis=-1, keepdims=True)
    xc = xf - mu
    var = jnp.mean(xc * xc, axis=-1, keepdims=True)
    y = xc * lax.rsqrt(var + LN_EPS) * g.astype(jnp.float32) + b.astype(jnp.float32)
    return y.astype(x.dtype)


def swiglu_ffn(x, w_in, w_out):
    gate, up = jnp.split(x @ w_in, 2, axis=-1)
    return (jax.nn.silu(gate) * up) @ w_out


def pool_mixer(x, w_pool, b_pool, scale):
    B, S, D = x.shape
    xf = x.astype(jnp.float32)
    c = jnp.concatenate([jnp.zeros((B, 1, D), jnp.float32), jnp.cumsum(xf, axis=1)], axis=1)
    t = jnp.arange(S)
    parts = []
    for g, w in enumerate(POOL_WINDOWS):
        cg = c[..., g * POOL_GROUP_DIM:(g + 1) * POOL_GROUP_DIM]
        lo = jnp.clip(t - w // 2, 0, S)
        hi = jnp.clip(t - w // 2 + w, 0, S)
        cnt = (hi - lo).astype(jnp.float32)
        win_sum = jnp.take(cg, hi, axis=1) - jnp.take(cg, lo, axis=1)
        parts.append(win_sum / cnt[None, :, None])
    pooled = jnp.concatenate(parts, axis=-1) - xf
    pooled = pooled.astype(x.dtype).reshape(B, S, N_POOL_GROUPS, POOL_GROUP_DIM)
    y = jnp.einsum('bsgc,gcd->bsgd', pooled, w_pool).reshape(B, S, D) + b_pool
    return y * scale


def conformer_conv(x, w_pw1, b_pw1, w_dw, b_dw, ln_g, ln_b, w_pw2, b_pw2):
    D = x.shape[-1]
    a, gate = jnp.split(x @ w_pw1 + b_pw1, 2, axis=-1)
    h = a * jax.nn.sigmoid(gate)
    h = lax.conv_general_dilated(
        h, w_dw[:, None, :], window_strides=(1,),
        padding=[(CONV_KERNEL // 2, CONV_KERNEL // 2)],
        dimension_numbers=('NWC', 'WIO', 'NWC'),
        feature_group_count=D) + b_dw
    h = jax.nn.silu(layernorm(h, ln_g, ln_b))
    return h @ w_pw2 + b_pw2


def setup_inputs(seed: int = 0) -> dict:
    key = jax.random.key(seed)
    ks = jax.random.split(key, 16)
    D, F, K = D_MODEL, D_FF, CONV_KERNEL
    nrm = lambda k, shape, s: jax.random.normal(k, shape, jnp.float32) * s
    return {
        "x": nrm(ks[0], (BATCH, SEQ, D), 1.0),
        "norm_g": 1.0 + nrm(ks[1], (DEPTH, 3, D), 0.05),
        "ffn_w_in": nrm(ks[2], (DEPTH, 2, D, 2 * F), D ** -0.5),
        "ffn_w_out": nrm(ks[3], (DEPTH, 2, F, D), F ** -0.5),
        "pool_w": nrm(ks[4], (N_POOL_LAYERS, N_POOL_GROUPS, POOL_GROUP_DIM, POOL_GROUP_DIM), POOL_GROUP_DIM ** -0.5),
        "pool_b": nrm(ks[5], (N_POOL_LAYERS, D), 0.02),
        "pool_scale": 1.0 + nrm(ks[6], (N_POOL_LAYERS, D), 0.1),
        "conv_w_pw1": nrm(ks[7], (N_CONV_LAYERS, D, 2 * D), D ** -0.5),
        "conv_b_pw1": nrm(ks[8], (N_CONV_LAYERS, 2 * D), 0.02),
        "conv_w_dw": nrm(ks[9], (N_CONV_LAYERS, K, D), K ** -0.5),
        "conv_b_dw": nrm(ks[10], (N_CONV_LAYERS, D), 0.02),
        "conv_ln_g": 1.0 + nrm(ks[11], (N_CONV_LAYERS, D), 0.05),
        "conv_ln_b": nrm(ks[12], (N_CONV_LAYERS, D), 0.02),
        "conv_w_pw2": nrm(ks[13], (N_CONV_LAYERS, D, D), D ** -0.5),
        "conv_b_pw2": nrm(ks[14], (N_CONV_LAYERS, D), 0.02),
        "final_g": 1.0 + nrm(ks[15], (D,), 0.05),
    }


def reference(x, norm_g, ffn_w_in, ffn_w_out, pool_w, pool_b, pool_scale,
              conv_w_pw1, conv_b_pw1, conv_w_dw, conv_b_dw, conv_ln_g, conv_ln_b,
              conv_w_pw2, conv_b_pw2, final_g):
    for i in range(DEPTH):
        x = x + FFN_RESIDUAL_WEIGHT * swiglu_ffn(rmsnorm(x, norm_g[i, 0]), ffn_w_in[i, 0], ffn_w_out[i, 0])
        h = rmsnorm(x, norm_g[i, 1])
        j = i // N_MIXERS
        if i % N_MIXERS == 0:
            x = x + pool_mixer(h, pool_w[j], pool_b[j], pool_scale[j])
        else:
            x = x + conformer_conv(h, conv_w_pw1[j], conv_b_pw1[j], conv_w_dw[j], conv_b_dw[j],
                                   conv_ln_g[j], conv_ln_b[j], conv_w_pw2[j], conv_b_pw2[j])
        x = x + FFN_RESIDUAL_WEIGHT * swiglu_ffn(rmsnorm(x, norm_g[i, 2]), ffn_w_in[i, 1], ffn_w_out[i, 1])
    return rmsnorm(x, final_g)
```

```python
import contextlib
import numpy as np
import concourse.bass as bass
import concourse.mybir as mybir
from concourse.bass_utils import run_bass_kernel_spmd

F32 = mybir.dt.float32
BF16 = mybir.dt.bfloat16
ALU = mybir.AluOpType
AF = mybir.ActivationFunctionType

D = 1024
DC = 8
DFF = 2816
FC = 22
DEPTH = 4
KCONV = 31
SEQ = 8192
BATCH = 4
NCORES = 8
NPASS = 2
VALID = 2048
HALO = 56
T = 432
NT = 5
E = T * NT
MARG = 16
EM = E + 2 * MARG
RMS_EPS = 1e-6
LN_EPS = 1e-5
GROUPS = [(0, 6), (6, 12), (12, 17), (17, 22)]
GMAX = 6
WIN_SZ = DC * 2 * GMAX * 128
WSET_SZ = WIN_SZ + GMAX * D
POOL_WINDOWS = (2, 4, 8, 16)
HW = T + 16
DGSZ = KCONV * 128
W2S_OFF = 4 * DGSZ
ARENA = 7280
NORM_OFF = ARENA - 1344

CL, CR = (56, 72), (2088, 2104)
ML, MR = (0, 64), (2096, 2160)
ENGS = ("pe", "act", "dve", "pool", "sp")
SEM_SPAN = 20000


def _cols_layout():
    off = {}
    n = 0

    def add(name, k):
        nonlocal n
        off[name] = n
        n += k

    add("ng", DEPTH * 3 * DC)
    add("fg", DC)
    add("pool_b", 2 * DC)
    add("pool_s", 2 * DC)
    add("b_pw1", 2 * 2 * DC)
    add("b_dw", 2 * DC)
    add("ln_g", 2 * DC)
    add("ln_b", 2 * DC)
    add("b_pw2", 2 * DC)
    add("w_dw", 2 * KCONV * DC)
    return off, n


COLS_OFF, NCOLS = _cols_layout()


class Chan:
    def __init__(self, sem):
        self.sem = sem
        self.count = 0


class Op:
    __slots__ = ("eng", "fn", "deps", "need", "chan", "sigval", "sem", "is_dma", "short")

    def __init__(self, eng, fn, chan):
        self.eng = eng
        self.fn = fn
        self.deps = []
        self.need = False
        self.chan = chan
        self.sigval = None
        self.sem = None
        self.is_dma = chan is not None
        self.short = False


class Sched:
    def __init__(self, same_engine_sync=True):
        self.streams = {e: [] for e in ENGS}
        self.lastw = {}
        self.rd_eng = {}
        self.rd_dma = {}
        self.same_engine_sync = same_engine_sync

    def add(self, eng, fn, reads=(), writes=(), chan=None, short=False):
        op = Op(eng, fn, chan)
        op.short = short
        deps = {}

        def dep(o):
            if o is None or o is op:
                return
            if o.is_dma:
                deps[id(o)] = (o, 16 * o.chan.count)
                return
            if o.eng == eng and not op.is_dma:
                if eng == "pe" or not (self.same_engine_sync or o.short or short):
                    return
            deps[id(o)] = (o, None)

        for k in reads:
            dep(self.lastw.get(k))
        for k in writes:
            dep(self.lastw.get(k))
            for o in self.rd_eng.get(k, {}).values():
                dep(o)
            for o in self.rd_dma.get(k, ()):
                dep(o)
        for k in writes:
            self.lastw[k] = op
            self.rd_eng[k] = {}
            self.rd_dma[k] = []
        for k in reads:
            if op.is_dma:
                self.rd_dma.setdefault(k, []).append(op)
            else:
                self.rd_eng.setdefault(k, {})[eng] = op
        if chan is not None:
            chan.count += 1
            op.sigval = 16 * chan.count
            op.sem = chan.sem
        op.deps = list(deps.values())
        self.streams[eng].append(op)
        return op

    def emit(self, block, eng_sems):
        for e in ENGS:
            for op in self.streams[e]:
                for d, _ in op.deps:
                    d.need = True
        for e in ENGS:
            cnt = 0
            for op in self.streams[e]:
                if op.is_dma:
                    continue
                if op.need:
                    si = cnt // SEM_SPAN
                    cnt += 1
                    op.sem = eng_sems[e][si]
                    op.sigval = cnt - si * SEM_SPAN
        self.stats = {}

        def run_stream(e):
            def body(eng):
                waited = {}
                nw = 0
                for op in self.streams[e]:
                    for d, v in op.deps:
                        val = d.sigval if v is None else v
                        key = id(d.sem)
                        if waited.get(key, 0) >= val:
                            continue
                        waited[key] = val
                        eng.wait_ge(d.sem, val)
                        nw += 1
                    if op.fn is None:
                        continue
                    ins = op.fn(eng)
                    if op.is_dma:
                        ins.then_inc(op.sem, 16)
                    elif op.need:
                        ins.then_inc(op.sem, 1)
                self.stats[e] = (len(self.streams[e]), nw)
            return body

        block.tensor(run_stream("pe"))
        block.scalar(run_stream("act"))
        block.vector(run_stream("dve"))
        block.gpsimd(run_stream("pool"))
        block.sync(run_stream("sp"))


class Builder:
    def __init__(self, n_pass=NPASS, layers=DEPTH, same_engine_sync=True, stop_after=None):
        self.n_pass = n_pass
        self.layers = layers
        self.stop_after = stop_after
        self.S = Sched(same_engine_sync)
        self.nc = bass.Bass("TRN2", target_bir_lowering=False)
        self.rot = {}
        self.fill_list = []
        self.fill_pos = 0
        self.fill_set = {}
        self.wtoggle = 0
        self.pending_fill = None
        self.cur_region = "AB"
        self.after_conv = False

    def dram_in(self, name, shape, dt=F32):
        return self.nc.dram_tensor(name, list(shape), dt, kind="ExternalInput").ap()

    def rotate(self, name, n):
        i = self.rot.get(name, 0)
        self.rot[name] = (i + 1) % n
        return i

    def col(self, name, idx):
        c = COLS_OFF[name] + idx
        return self.cols[:, c:c + 1]

    def build(self):
        nc = self.nc
        npass = self.n_pass
        self.xT = self.dram_in("xT", [npass, D, E])
        self.maskd = self.dram_in("mask", [npass, 128, E])
        self.cntd = self.dram_in("cnt", [npass, 4, 128, E])
        self.colsd = self.dram_in("cols", [128, NCOLS])
        self.identd = self.dram_in("ident", [128, 128])
        self.w_in = self.dram_in("ffn_w_in", [DEPTH, 2, D, 2 * DFF])
        self.w_out = self.dram_in("ffn_w_out", [DEPTH, 2, DFF, D])
        self.pool_w = self.dram_in("pool_w", [2, 4, 256, 256])
        self.w_pw1 = self.dram_in("conv_w_pw1", [2, D, 2 * D])
        self.w_pw2 = self.dram_in("conv_w_pw2", [2, D, D])
        self.yT = nc.dram_tensor("yT", [npass, D, E], F32, kind="ExternalOutput").ap()

        with contextlib.ExitStack() as st:
            def sb(name, shape, dt):
                return st.enter_context(nc.sbuf_tensor(name, list(shape), dt))

            self.x = sb("x_sb", [128, DC * E], F32)
            self.xn = sb("xn_sb", [128, DC * EM], BF16)
            self.wset = [sb("wset0", [128, WSET_SZ], BF16), sb("wset1", [128, WSET_SZ], BF16)]
            self.cols = sb("cols_sb", [128, NCOLS], F32)
            self.g32 = sb("g32_sb", [128, (DEPTH * 3 + 1) * DC + 2 * DC], F32)
            self.bndc = sb("bndc_sb", [128, 2 * 4 * 16], F32)
            self.bndm = sb("bndm_sb", [128, 2 * 64], F32)
            self.ones = sb("ones_sb", [128, 128], BF16)
            self.ident = sb("ident_sb", [128, 128], BF16)
            self.scr = sb("scr_sb", [128, 2], F32)
            self.epsc = sb("eps_sb", [128, 2], F32)
            self.arena = sb("arena", [128, ARENA], F32)
            self.ps = [st.enter_context(nc.psum_tensor(f"ps{i}", [128, 512], F32)) for i in range(8)]
            sems = {e: [st.enter_context(nc.semaphore(f"s_{e}{i}")) for i in range(4)] for e in ENGS}

            def ch(name):
                return Chan(st.enter_context(nc.semaphore(name)))

            self.ch_x = [ch(f"c_x{t}") for t in range(NT)]
            self.ch_w = [ch("c_w0"), ch("c_w1")]
            self.ch_m = [ch("c_m0"), ch("c_m1")]
            self.ch_c = ch("c_c")
            self.ch_y = ch("c_y")
            self.ch_k = ch("c_k")

            self.program()

            with nc.Block() as block:
                self.S.emit(block, sems)
        return nc

    def xv(self, c, a, b):
        return self.x[:, c * E + a: c * E + b]

    def xnv(self, c, a, b):
        return self.xn[:, c * EM + MARG + a: c * EM + MARG + b]

    def ar32(self, off, n):
        return self.arena[:, off:off + n]

    def ar16(self, off32, n16):
        v = self.arena[:, off32:off32 + (n16 + 1) // 2].bitcast(BF16)
        return v[:, 0:n16]

    @staticmethod
    def tkeys(name, c, a, b):
        t0 = max(a, 0) // T
        t1 = (min(b, E) - 1) // T
        return [(name, c, t) for t in range(t0, t1 + 1)]

    def A(self, eng, fn, reads=(), writes=(), chan=None, short=False, region=None):
        region = region or self.cur_region
        return self.S.add(eng, fn, list(reads) + ["arena" + r for r in region], writes, chan, short)

    def mask_fix(self, view_fn, t, keys):
        if t == 0:
            side, (a, b) = 0, ML
        elif t == NT - 1:
            side, (a, b) = 1, MR
        else:
            return
        v = view_fn(a, b)
        m = self.bndm[:, side * 64:(side + 1) * 64]
        self.S.add("dve", lambda e: e.tensor_tensor(out=v, in0=v, in1=m, op=ALU.mult),
                   reads=list(keys) + [("bndm", side)], writes=list(keys), short=True)

    def phase_begin(self, regions="ABN"):
        self.cur_region = regions
        self.S.add("dve", lambda e: e.memset(self.scr[:, 0:1], 0.0), writes=["arena" + r for r in regions], short=True)

    def plan_fills(self):
        fl = []
        for p in range(self.n_pass):
            for l in range(self.layers):
                for g in range(len(GROUPS)):
                    fl.append(("ffn", p, l, 0, g))
                if self.stop_after == (l, 0):
                    break
                if l % 2 == 0:
                    fl.append(("poolw", p, l))
                else:
                    fl.append(("w1", p, l))
                    fl.append(("w2", p, l))
                if self.stop_after == (l, 1):
                    break
                for g in range(len(GROUPS)):
                    fl.append(("ffn", p, l, 1, g))
                if self.stop_after == (l, 2):
                    break
        self.fill_list = fl

    def issue_next_fill(self, part=None, extra_writes=()):
        if part == "out":
            if self.pending_fill is None:
                return
            f, s = self.pending_fill
            self.pending_fill = None
        else:
            if self.fill_pos >= len(self.fill_list):
                return
            f = self.fill_list[self.fill_pos]
            self.fill_pos += 1
            s = self.wtoggle
            self.wtoggle ^= 1
            self.fill_set[f] = s
            if part == "in":
                assert f[0] == "ffn"
                self.pending_fill = (f, s)
        S = self.S
        ws = self.wset[s]
        chn = self.ch_w[s]
        kind = f[0]
        first = [part != "out"]

        xw = [list(extra_writes)]

        def dma(dst, src):
            wr = ([("wset", s)] if first[0] else []) + xw[0]
            first[0] = False
            xw[0] = []
            S.add("pool", lambda e: e.dma_start(out=dst, in_=src), writes=wr, chan=chn)

        if kind == "ffn":
            _, p, l, i, g = f
            f0, f1 = GROUPS[g]
            n = f1 - f0
            if part != "out":
                dstw = ws[:, 0:WIN_SZ].rearrange("p (kc h j) -> p kc h j", kc=DC, h=2, j=GMAX * 128)
                for half in range(2):
                    src = self.w_in[l, i, :, half * DFF + f0 * 128: half * DFF + f1 * 128].rearrange("(kc p) f -> p kc f", p=128)
                    dst = dstw[:, :, half, 0:n * 128]
                    dma(dst, src)
            if part != "in":
                src = self.w_out[l, i, f0 * 128:f1 * 128, :].rearrange("(n p) d -> p n d", p=128)
                dst = ws[:, WIN_SZ:WIN_SZ + n * D].rearrange("p (n d) -> p n d", d=D)
                dma(dst, src)
        elif kind == "poolw":
            _, p, l = f
            j = l // 2
            src = self.pool_w[j].rearrange("g (kc p) m -> p g kc m", p=128)
            dst = ws[:, 10752:10752 + 2048].rearrange("p (g kc m) -> p g kc m", g=4, kc=2)
            dma(dst, src)
        elif kind == "w1":
            _, p, l = f
            j = l // 2
            src = self.w_pw1[j].rearrange("(kc p) m -> p kc m", p=128)
            dst = ws[:, 0:DC * 2 * D].rearrange("p (kc m) -> p kc m", kc=DC)
            dma(dst, src)
        elif kind == "w2":
            _, p, l = f
            j = l // 2
            src = self.w_pw2[j][2 * 128:4 * 128, :].rearrange("(kc p) m -> p kc m", p=128)
            dst = ws[:, W2S_OFF:W2S_OFF + 2 * D].rearrange("p (kc m) -> p kc m", kc=2)
            dma(dst, src)

    def program(self):
        S = self.S
        self.plan_fills()
        S.add("sp", lambda e: e.dma_start(out=self.cols[:], in_=self.colsd[:, :]), writes=["cols"], chan=self.ch_k)
        self.issue_next_fill()
        S.add("dve", lambda e: e.memset(self.ones[:], 1.0), writes=["ones"], short=True)
        S.add("dve", lambda e: e.memset(self.epsc[:, 0:1], float(RMS_EPS)), writes=["epsc"], short=True)
        S.add("dve", lambda e: e.memset(self.epsc[:, 1:2], float(LN_EPS)), reads=["epsc"], writes=["epsc"], short=True)
        S.add("dve", lambda e: e.memset(self.xn[:], 0.0),
              writes=[("xn", c, t) for c in range(DC) for t in range(NT)])
        ng0 = COLS_OFF["ng"]
        S.add("dve", lambda e: e.tensor_scalar(out=self.g32[:, 0:DEPTH * 3 * DC], in0=self.cols[:, ng0:ng0 + DEPTH * 3 * DC],
                                               scalar1=1.0, scalar2=None, op0=ALU.mult), reads=["cols"], writes=["g32"], short=True)
        fg0 = COLS_OFF["fg"]
        S.add("dve", lambda e: e.tensor_scalar(out=self.g32[:, DEPTH * 3 * DC:(DEPTH * 3 + 1) * DC], in0=self.cols[:, fg0:fg0 + DC],
                                               scalar1=1.0, scalar2=None, op0=ALU.mult), reads=["cols"], writes=["g32f"], short=True)
        S.add("pool", lambda e: e.dma_start(out=self.ident[:], in_=self.identd[:, :]), writes=["ident"], chan=self.ch_m[0])
        BS0 = (DEPTH * 3 + 1) * DC
        pb0, ps0 = COLS_OFF["pool_b"], COLS_OFF["pool_s"]
        S.add("dve", lambda e: e.tensor_tensor(out=self.g32[:, BS0:BS0 + 2 * DC], in0=self.cols[:, pb0:pb0 + 2 * DC],
                                               in1=self.cols[:, ps0:ps0 + 2 * DC], op=ALU.mult), reads=["cols"], writes=["bs"], short=True)

        for p in range(self.n_pass):
            self.p = p
            for side, (ca_, cb_) in enumerate((CL, CR)):
                dst = self.bndc[:, side * 64:(side + 1) * 64].rearrange("p (g n) -> p g n", g=4)
                S.add("sp", (lambda dst=dst, ca_=ca_, cb_=cb_, p=p: lambda e: e.dma_start(
                    out=dst, in_=self.cntd[p, :, :, ca_:cb_].rearrange("g p n -> p g n")))(),
                    writes=[("bndc", side)], chan=self.ch_c)
            for side, (ma_, mb_) in enumerate((ML, MR)):
                S.add("sp", (lambda side=side, ma_=ma_, mb_=mb_, p=p: lambda e: e.dma_start(
                    out=self.bndm[:, side * 64:(side + 1) * 64], in_=self.maskd[p, :, ma_:mb_]))(),
                    writes=[("bndm", side)], chan=self.ch_c)
            for t in range(NT):
                for c in range(DC):
                    S.add("sp", (lambda c=c, p=p, t=t: lambda e: e.dma_start(
                        out=self.xv(c, t * T, (t + 1) * T), in_=self.xT[p, c * 128:(c + 1) * 128, t * T:(t + 1) * T]))(),
                        writes=[("x", c, t)], chan=self.ch_x[t])
            done = False
            for l in range(self.layers):
                self.ffn(p, l, 0)
                if self.stop_after == (l, 0):
                    done = True
                    break
                hook = None
                if l % 2 == 0:
                    hook = self.pool_mixer(p, l, defer=False)
                else:
                    self.conv_mixer(p, l)
                if self.stop_after == (l, 1):
                    done = True
                    break
                self.ffn(p, l, 1, hook=hook)
                if self.stop_after == (l, 2):
                    done = True
                    break
            self.final_norm(p, raw=done)
        S.add("sp", None, reads=[("y", p, t, c) for p in range(self.n_pass) for t in range(NT) for c in range(DC)])

    def rms_rstd(self, a, b):
        n = b - a
        i = self.rotate("rstd", 2)
        rstd = self.ar32(NORM_OFF + i * 448, n)
        for c in range(DC):
            j = self.rotate("sq", 2)
            sq = self.ar16(NORM_OFF + 896 + j * 224, n)
            self.A("act", (lambda sq=sq, c=c: lambda e: e.activation(out=sq, in_=self.xv(c, a, b), func=AF.Square))(),
                   reads=self.tkeys("x", c, a, b), writes=[("sq", j)], region="N")
            self.A("pe", (lambda sq=sq, c=c: lambda e: e.matmul(self.ps[7][:, 0:n], lhsT=self.ones[:], rhs=sq,
                                                              start=(c == 0), stop=(c == DC - 1)))(),
                   reads=[("sq", j), "ones"], writes=[("ps", 7)], region="N")
        self.A("act", lambda e: e.activation(out=rstd, in_=self.ps[7][:, 0:n], func=AF.Sqrt, bias=self.epsc[:, 0:1], scale=1.0 / D),
               reads=[("ps", 7), "epsc"], writes=[("rstd", i)], region="N")
        self.A("dve", lambda e: e.reciprocal(out=rstd, in_=rstd), reads=[("rstd", i)], writes=[("rstd", i)], region="N")
        return rstd, ("rstd", i)

    def g32col(self, idx):
        return self.g32[:, idx:idx + 1]

    def ffn_norm_tile(self, gbase, t, t0, t1):
        rstd, rk = self.rms_rstd(t0, t1)
        for c in range(DC):
            self.A("dve", (lambda c=c: lambda e: e.scalar_tensor_tensor(
                out=self.xnv(c, t0, t1), in0=self.xv(c, t0, t1), scalar=self.g32col(gbase + c), in1=rstd,
                op0=ALU.mult, op1=ALU.mult))(),
                reads=[("x", c, t), rk, "g32"], writes=[("xn", c, t)], region="N")

    def ffn(self, p, l, i, hook=None):
        if hook is None:
            self.phase_begin("AN" if self.after_conv else "A")
            self.after_conv = False
        gbase = (l * 3 + (0 if i == 0 else 2)) * DC
        steps = [(g, t) for g in range(len(GROUPS)) for t in range(NT)]
        full = (self.layers == DEPTH and self.stop_after is None)
        m = ((46, 38, 23, 15)[l] if i == 0 else (38, 23, 15, 0)[l]) if full else HALO
        lo_trim = (HALO - m) // 4 * 4
        hi_end = min(T, -(-(E - HALO + m - (NT - 1) * T) // 4) * 4)

        def trange(t):
            return t * T + (lo_trim if t == 0 else 0), (t * T + hi_end) if t == NT - 1 else (t + 1) * T

        def GT(gs, fi):
            return self.ar16(gs * (GMAX * T // 2) + fi * (T // 2), T)

        def SG(j):
            return self.ar32(2 * GMAX * T // 2 + j * T, T)

        def do_H(k):
            g, t = steps[k]
            if g == 0 and t + 1 < NT:
                if hook is not None:
                    hook(t + 1)
                    if t + 1 == NT - 1:
                        self.issue_next_fill()
                self.ffn_norm_tile(gbase, t + 1, *trange(t + 1))
            s = self.fill_set[("ffn", p, l, i, g)]
            f0, f1 = GROUPS[g]
            n = f1 - f0
            t0, t1 = trange(t)
            nt = t1 - t0
            gs = k % 2
            win = self.wset[s][:, 0:WIN_SZ].rearrange("p (kc h j) -> p kc h j", kc=DC, h=2, j=GMAX * 128)
            for fi in range(n):
                r = self.rotate("hb", 2)
                bg, bu = r, 2 + r
                for half, bank in ((0, bg), (1, bu)):
                    for kc in range(DC):
                        self.S.add("pe", (lambda kc=kc, half=half, bank=bank, fi=fi: lambda e: e.matmul(
                            self.ps[bank][:, 0:nt], lhsT=win[:, kc, half, fi * 128:(fi + 1) * 128],
                            rhs=self.xnv(kc, t0, t1), start=(kc == 0), stop=(kc == DC - 1)))(),
                            reads=[("wset", s), ("xn", kc, t)], writes=[("ps", bank)])
                j = self.rotate("sg", 2)
                sg = SG(j)
                self.A("act", (lambda sg=sg, bg=bg: lambda e: e.activation(out=sg[:, 0:nt], in_=self.ps[bg][:, 0:nt], func=AF.Silu))(),
                       reads=[("ps", bg)], writes=[("sg", j)])
                gt = GT(gs, fi)
                self.A("dve", (lambda gt=gt, sg=sg, bu=bu: lambda e: e.tensor_tensor(
                    out=gt[:, 0:nt], in0=self.ps[bu][:, 0:nt], in1=sg[:, 0:nt], op=ALU.mult))(),
                    reads=[("ps", bu), ("sg", j)], writes=[("gt", gs, fi)])

        def do_OUT(k):
            g, t = steps[k]
            s = self.fill_set[("ffn", p, l, i, g)]
            f0, f1 = GROUPS[g]
            n = f1 - f0
            t0, t1 = trange(t)
            nt = t1 - t0
            gs = k % 2
            wout = self.wset[s][:, WIN_SZ:WIN_SZ + GMAX * D].rearrange("p (n d) -> p n d", d=D)
            for dc in range(DC):
                bo = 4 + self.rotate("ob", 3)
                for fi in range(n):
                    self.A("pe", (lambda fi=fi, dc=dc, bo=bo: lambda e: e.matmul(
                        self.ps[bo][:, 0:nt], lhsT=wout[:, fi, dc * 128:(dc + 1) * 128], rhs=GT(gs, fi)[:, 0:nt],
                        start=(fi == 0), stop=(fi == n - 1)))(),
                        reads=[("wset", s), ("gt", gs, fi)], writes=[("ps", bo)])
                self.S.add("dve", (lambda dc=dc, bo=bo: lambda e: e.scalar_tensor_tensor(
                    out=self.xv(dc, t0, t1), in0=self.ps[bo][:, 0:nt], scalar=0.5, in1=self.xv(dc, t0, t1),
                    op0=ALU.mult, op1=ALU.add))(),
                    reads=[("ps", bo), ("x", dc, t)], writes=[("x", dc, t)])

        if hook is None:
            self.issue_next_fill()
        else:
            hook(0)
        self.ffn_norm_tile(gbase, 0, *trange(0))
        do_H(0)
        for k in range(len(steps)):
            if k + 1 < len(steps):
                do_H(k + 1)
            do_OUT(k)
            if steps[k][1] == NT - 1 and k + 1 < len(steps):
                self.issue_next_fill()

    def load_mask(self, p, t):
        raise NotImplementedError

    def pool_mixer(self, p, l, defer=True):
        S = self.S
        self.phase_begin("B")
        j = l // 2
        s = self.fill_set[("poolw", p, l)]
        self.issue_next_fill()
        wsf = self.wset[s][:, :].bitcast(F32)
        WS = ("wset", s)

        def hv(c, a, b):
            return wsf[:, c * HW + a: c * HW + b]

        def stmp(k, a, b):
            return wsf[:, 8 * HW + k * HW + a: 8 * HW + k * HW + b]

        pw = self.wset[s][:, 10752:10752 + 2048].rearrange("p (g kc m) -> p g kc m", g=4, kc=2)
        pooled = lambda c: self.wset[s][:, 12800 + c * T: 12800 + (c + 1) * T]
        ytmp = lambda k: self.ar32(3456 + k * T, T)
        tmpf = lambda k: self.ar32(4320 + 16 * k, 16)
        BS0 = (DEPTH * 3 + 1) * DC

        def tile(t):
            t0, t1 = t * T, (t + 1) * T
            ca, cb = t0, min(t1 + 8, E)
            lo = 8
            n = cb - ca
            rstd, rk = self.rms_rstd(ca, cb)
            for c in range(DC):
                if t == 0:
                    S.add("dve", (lambda c=c: lambda e: e.memset(hv(c, 0, 8), 0.0))(), reads=[WS], writes=[("h", c)], short=True)
                else:
                    S.add("dve", (lambda c=c: lambda e: e.tensor_copy(out=hv(c, 0, 8), in_=hv(c, T, T + 8)))(),
                          reads=[WS, ("h", c)], writes=[("h", c)], short=True)
            for c in range(DC):
                S.add("dve", (lambda c=c: lambda e: e.scalar_tensor_tensor(
                    out=hv(c, lo, lo + n), in0=self.xv(c, ca, cb), scalar=self.g32col((l * 3 + 1) * DC + c), in1=rstd,
                    op0=ALU.mult, op1=ALU.mult))(),
                    reads=self.tkeys("x", c, ca, cb) + [rk, "g32", WS, "arenaB", "arenaN", ("h", c)], writes=[("h", c)])
            if lo + n < HW:
                for c in range(DC):
                    S.add("dve", (lambda c=c: lambda e: e.memset(hv(c, lo + n, HW), 0.0))(), reads=[WS], writes=[("h", c)], short=True)
            lohi = [(1, HW, 1, 0), (2, HW - 1, 1, 1), (4, HW - 3, 2, 2), (8, HW - 7, 4, 4)]
            for g in range(4):
                pair = (2 * g, 2 * g + 1)
                levels = g + 1
                kbs = {pair[0]: 0, pair[1]: 2}
                srcs = {c: (lambda c: (lambda a_, b_: hv(c, a_, b_)))(c) for c in pair}
                srcks = {c: ("h", c) for c in pair}
                for lv in range(levels):
                    ja, jb, dl, dr = lohi[lv]
                    for c in pair:
                        dk = kbs[c] + (lv % 2)
                        dst = stmp(dk, ja, jb)
                        in0 = srcs[c](ja - dl, jb - dl)
                        in1 = srcs[c](ja + dr, jb + dr)
                        S.add("dve", (lambda dst=dst, in0=in0, in1=in1: lambda e: e.tensor_tensor(out=dst, in0=in0, in1=in1, op=ALU.add))(),
                              reads=[srcks[c], WS], writes=[("stmp", dk)])
                        srcs[c] = (lambda dk: (lambda a_, b_: stmp(dk, a_, b_)))(dk)
                        srcks[c] = ("stmp", dk)
                for c in pair:
                    fsrc = srcs[c](8, 8 + T)
                    S.add("dve", (lambda fsrc=fsrc, c=c, g=g: lambda e: e.scalar_tensor_tensor(
                        out=pooled(c), in0=fsrc, scalar=1.0 / POOL_WINDOWS[g], in1=hv(c, 8, 8 + T),
                        op0=ALU.mult, op1=ALU.subtract))(),
                        reads=[srcks[c], ("h", c), WS, "arenaB"], writes=[("pooled", c)])
                if t in (0, NT - 1):
                    side, (wa, wb) = (0, CL) if t == 0 else (1, CR)
                    la, lb = wa - t0, wb - t0
                    icv = self.bndc[:, side * 64 + g * 16: side * 64 + (g + 1) * 16]
                    for k, c in enumerate(pair):
                        fs2 = srcs[c](8 + la, 8 + lb)
                        tf = tmpf(k)
                        S.add("dve", (lambda fs2=fs2, icv=icv, tf=tf: lambda e: e.tensor_tensor(out=tf, in0=fs2, in1=icv, op=ALU.mult))(),
                              reads=[srcks[c], ("bndc", side), WS, "arenaB"], writes=[("tmpf", k)], short=True)
                    for k, c in enumerate(pair):
                        tf = tmpf(k)
                        S.add("dve", (lambda c=c, la=la, lb=lb, tf=tf: lambda e: e.tensor_tensor(
                            out=pooled(c)[:, la:lb], in0=tf, in1=hv(c, 8 + la, 8 + lb), op=ALU.subtract))(),
                            reads=[("tmpf", k), ("h", c), WS, "arenaB", ("pooled", c)], writes=[("pooled", c)], short=True)
            for g in range(4):
                for mc in range(2):
                    c = 2 * g + mc
                    bo = 4 + self.rotate("ob", 3)
                    for kc in range(2):
                        S.add("pe", (lambda g=g, mc=mc, kc=kc, bo=bo: lambda e: e.matmul(
                            self.ps[bo][:, 0:T], lhsT=pw[:, g, kc, mc * 128:(mc + 1) * 128], rhs=pooled(2 * g + kc),
                            start=(kc == 0), stop=(kc == 1)))(),
                            reads=[WS, ("pooled", 2 * g + kc), "arenaB"], writes=[("ps", bo)])
                    yk = self.rotate("ytmp", 2)
                    self.A("act", (lambda c=c, bo=bo, yk=yk: lambda e: e.activation(
                        out=ytmp(yk), in_=self.ps[bo][:, 0:T], func=AF.Identity,
                        bias=self.g32[:, BS0 + j * DC + c: BS0 + j * DC + c + 1], scale=self.col("pool_s", j * DC + c)))(),
                        reads=[("ps", bo), "bs", "cols"], writes=[("ytmp", yk)])
                    self.A("dve", (lambda c=c, yk=yk: lambda e: e.tensor_tensor(
                        out=self.xv(c, t0, t1), in0=self.xv(c, t0, t1), in1=ytmp(yk), op=ALU.add))(),
                        reads=[("ytmp", yk), ("x", c, t)], writes=[("x", c, t)])
                    self.mask_fix(lambda a_, b_, c=c: self.xv(c, a_, b_), t, [("x", c, t)])

        if defer:
            return tile
        for t in range(NT):
            tile(t)
        return None

    def conv_mixer(self, p, l):
        S = self.S
        j = l // 2
        self.phase_begin("AB")
        sA = self.fill_set[("w1", p, l)]
        self.issue_next_fill()
        for c in range(DC):
            S.add("dve", (lambda c=c: lambda e: e.memset(self.xnv(c, -MARG, 0), 0.0))(), writes=[("R", c, 0)], short=True)
        WA = ("wset", sA)
        w1 = self.wset[sA][:, 0:DC * 2 * D].rearrange("p (kc m) -> p kc m", kc=DC)
        xnt = lambda k, c: self.ar16(k * (DC * T // 2) + c * (T // 2), T)
        sgm = lambda k: self.ar32(3456 + k * T, T)
        glv = lambda k: self.ar32(4320 + k * T, T)
        maskA = lambda k: self.ar16(5184 + k * (T // 2), T)
        def normA(t):
            t0, t1 = t * T, (t + 1) * T
            mk = 0
            rstd, rk = self.rms_rstd(t0, t1)
            xk = self.rotate("xnt", 2)
            for c in range(DC):
                self.A("dve", (lambda c=c, xk=xk: lambda e: e.scalar_tensor_tensor(
                    out=xnt(xk, c), in0=self.xv(c, t0, t1), scalar=self.g32col((l * 3 + 1) * DC + c), in1=rstd,
                    op0=ALU.mult, op1=ALU.mult))(),
                    reads=[("x", c, t), rk, "g32"], writes=[("xnt", xk, c)], region="ABN")
            return xk, mk

        def tileA(t, xk, mk):
            t0, t1 = t * T, (t + 1) * T
            for mc in range(DC):
                r = self.rotate("hb", 2)
                ba, bg = r, 2 + r
                for half, bank in ((0, ba), (1, bg)):
                    for kc in range(DC):
                        self.A("pe", (lambda kc=kc, half=half, bank=bank, mc=mc, xk=xk: lambda e: e.matmul(
                            self.ps[bank][:, 0:T], lhsT=w1[:, kc, half * D + mc * 128: half * D + (mc + 1) * 128],
                            rhs=xnt(xk, kc), start=(kc == 0), stop=(kc == DC - 1)))(),
                            reads=[WA, ("xnt", xk, kc)], writes=[("ps", bank)])
                sk = self.rotate("sgm", 2)
                self.A("act", (lambda sk=sk, bg=bg, mc=mc: lambda e: e.activation(
                    out=sgm(sk), in_=self.ps[bg][:, 0:T], func=AF.Sigmoid, bias=self.col("b_pw1", j * 2 * DC + DC + mc)))(),
                    reads=[("ps", bg), "cols"], writes=[("sgm", sk)])
                self.A("dve", (lambda sk=sk, ba=ba, mc=mc: lambda e: e.scalar_tensor_tensor(
                    out=self.xnv(mc, t0, t1), in0=self.ps[ba][:, 0:T], scalar=self.col("b_pw1", j * 2 * DC + mc), in1=sgm(sk),
                    op0=ALU.add, op1=ALU.mult))(),
                    reads=[("ps", ba), ("sgm", sk), "cols"], writes=[("xn", mc, t)])
                self.mask_fix(lambda a_, b_, mc=mc: self.xnv(mc, a_, b_), t, [("xn", mc, t)])

        sB = self.fill_set[("w2", p, l)]
        WB = ("wset", sB)
        dgv = lambda c, tap: (self.wset[sA] if c < 4 else self.wset[sB])[:, (c % 4) * DGSZ + tap * 128:(c % 4) * DGSZ + (tap + 1) * 128]

        def build_diags(c, first_in_set=False):
            for tap in range(KCONV):
                wr = [("dgr", c, tap)]
                rd = ["ident", "cols"]
                if c < 4:
                    if first_in_set and tap == 0:
                        wr.append(WA)
                    else:
                        rd.append(WA)
                else:
                    rd.append(WB)
                S.add("dve", (lambda tap=tap: lambda e: e.tensor_scalar(
                    out=dgv(c, tap), in0=self.ident[:], scalar1=self.col("w_dw", j * KCONV * DC + tap * DC + c), scalar2=None,
                    op0=ALU.mult))(), reads=rd, writes=wr)

        nxt = normA(0)
        for t in range(NT):
            cur = nxt
            if t + 1 < NT:
                nxt = normA(t + 1)
            tileA(t, *cur)
            if t < 4:
                build_diags(4 + t)

        self.phase_begin()
        co = lambda c: self.ar32(c * T, T)
        co16 = self.ar16(3456, T)
        sq16 = self.ar16(3456 + T // 2, T)
        w2a = lambda kc: self.ar16(3888 + (kc - 4) * (D // 2), D)
        mean = self.ar32(NORM_OFF, T)
        var = self.ar32(NORM_OFF + T, T)
        rsd = self.ar32(NORM_OFF + 2 * T, T)
        zv = lambda c, t: self.xnv(c, t * T - 16, (t + 1) * T - 16)

        def w2v(kc, mc):
            if kc < 2:
                return self.wset[sA][:, W2S_OFF + kc * D + mc * 128: W2S_OFF + kc * D + (mc + 1) * 128]
            if kc < 4:
                return self.wset[sB][:, W2S_OFF + (kc - 2) * D + mc * 128: W2S_OFF + (kc - 2) * D + (mc + 1) * 128]
            return w2a(kc)[:, mc * 128:(mc + 1) * 128]

        build_diags(0, first_in_set=True)
        srcA = self.w_pw2[j][0:2 * 128, :].rearrange("(kc p) m -> p kc m", p=128)
        dstA = self.wset[sA][:, W2S_OFF:W2S_OFF + 2 * D].rearrange("p (kc m) -> p kc m", kc=2)
        S.add("pool", lambda e: e.dma_start(out=dstA, in_=srcA), reads=[WA], writes=[("w2s", 0), ("w2s", 1)], chan=self.ch_w[sA])
        for kc in range(4, DC):
            self.A("pool", (lambda kc=kc: lambda e: e.dma_start(out=w2a(kc), in_=self.w_pw2[j][kc * 128:(kc + 1) * 128, :]))(),
                   writes=[("w2s", kc)], chan=self.ch_m[1])
        for c in range(1, 4):
            build_diags(c)

        def B1(t):
            t0, t1 = t * T, (t + 1) * T

            def stats(c):
                self.A("pe", (lambda c=c: lambda e: e.matmul(self.ps[6][:, 0:T], lhsT=self.ones[:], rhs=co16,
                                                            start=(c == 0), stop=(c == DC - 1)))(),
                       reads=["co16", "ones"], writes=[("ps", 6)])
                self.A("pe", (lambda c=c: lambda e: e.matmul(self.ps[7][:, 0:T], lhsT=self.ones[:], rhs=sq16,
                                                            start=(c == 0), stop=(c == DC - 1)))(),
                       reads=["sq16", "ones"], writes=[("ps", 7)])

            for c in range(DC):
                bo = 4 + self.rotate("cvb", 2)
                rkeys = [("R", c, t), ("R", c, t + 1)] + self.tkeys("xn", c, t0 - 15, t1 + 15)
                if c >= 4:
                    rkeys.append(WB)
                for tap in range(KCONV):
                    S.add("pe", (lambda tap=tap, c=c, bo=bo: lambda e: e.matmul(
                        self.ps[bo][:, 0:T], lhsT=dgv(c, tap), rhs=self.xnv(c, t0 + tap - 15, t1 + tap - 15),
                        start=(tap == 0), stop=(tap == KCONV - 1)))(),
                        reads=rkeys + [("dgr", c, tap)], writes=[("ps", bo)])
                if c > 0:
                    stats(c - 1)
                bcol = self.col("b_dw", j * DC + c)
                self.A("act", (lambda c=c, bo=bo, bcol=bcol: lambda e: e.activation(
                    out=co(c), in_=self.ps[bo][:, 0:T], func=AF.Identity, bias=bcol))(),
                    reads=[("ps", bo), "cols"], writes=[("co", c)])
                self.A("act", (lambda bo=bo, bcol=bcol: lambda e: e.activation(
                    out=co16, in_=self.ps[bo][:, 0:T], func=AF.Identity, bias=bcol))(),
                    reads=[("ps", bo), "cols"], writes=["co16"])
                self.A("act", (lambda bo=bo, bcol=bcol: lambda e: e.activation(
                    out=sq16, in_=self.ps[bo][:, 0:T], func=AF.Square, bias=bcol))(),
                    reads=[("ps", bo), "cols"], writes=["sq16"])
                if t == NT - 1 and c == 3:
                    self.issue_next_fill(part="in", extra_writes=[("dgr", cc, tp) for cc in range(4) for tp in range(KCONV)])
            stats(DC - 1)

        def LN(t):
            self.A("dve", lambda e: e.tensor_scalar(out=mean, in0=self.ps[6][:, 0:T], scalar1=1.0 / D, scalar2=None, op0=ALU.mult),
                   reads=[("ps", 6)], writes=["mean"])
            self.A("dve", lambda e: e.tensor_tensor(out=var, in0=mean, in1=mean, op=ALU.mult),
                   reads=["mean"], writes=["var"])
            self.A("dve", lambda e: e.scalar_tensor_tensor(out=var, in0=self.ps[7][:, 0:T], scalar=1.0 / D, in1=var,
                                                           op0=ALU.mult, op1=ALU.subtract),
                   reads=[("ps", 7), "var"], writes=["var"])
            self.A("act", lambda e: e.activation(out=rsd, in_=var, func=AF.Sqrt, bias=self.epsc[:, 1:2], scale=1.0),
                   reads=["var", "epsc"], writes=["rsd"])
            self.A("dve", lambda e: e.reciprocal(out=rsd, in_=rsd), reads=["rsd"], writes=["rsd"])

        def LN_main(t):
            for c0 in range(0, DC, 2):
                for c in (c0, c0 + 1):
                    self.A("dve", (lambda c=c: lambda e: e.tensor_tensor(out=co(c), in0=co(c), in1=mean, op=ALU.subtract))(),
                           reads=[("co", c), "mean"], writes=[("co", c)])
                for c in (c0, c0 + 1):
                    self.A("dve", (lambda c=c: lambda e: e.tensor_tensor(out=co(c), in0=co(c), in1=rsd, op=ALU.mult))(),
                           reads=[("co", c), "rsd"], writes=[("co", c)])
                for c in (c0, c0 + 1):
                    self.A("act", (lambda c=c: lambda e: e.activation(
                        out=zv(c, t), in_=co(c), func=AF.Silu, bias=self.col("ln_b", j * DC + c), scale=self.col("ln_g", j * DC + c)))(),
                        reads=[("co", c), "cols"], writes=[("R", c, t)])

        def PW2(t):
            t0, t1 = t * T, (t + 1) * T
            for mc in range(DC):
                bo = self.rotate("p2b", 4)
                for kc in range(DC):
                    rk = [("R", kc, t), ("w2s", kc), "arenaA", "arenaB"] + self.tkeys("xn", kc, t0 - 16, t1 - 16)
                    if 2 <= kc < 4:
                        rk.append(WB)
                    S.add("pe", (lambda kc=kc, mc=mc, bo=bo: lambda e: e.matmul(
                        self.ps[bo][:, 0:T], lhsT=w2v(kc, mc), rhs=zv(kc, t),
                        start=(kc == 0), stop=(kc == DC - 1)))(),
                        reads=rk, writes=[("ps", bo)])
                S.add("dve", (lambda mc=mc, bo=bo: lambda e: e.scalar_tensor_tensor(
                    out=self.xv(mc, t0, t1), in0=self.ps[bo][:, 0:T], scalar=self.col("b_pw2", j * DC + mc), in1=self.xv(mc, t0, t1),
                    op0=ALU.add, op1=ALU.add))(),
                    reads=[("ps", bo), ("x", mc, t), "cols"], writes=[("x", mc, t)])
                self.mask_fix(lambda a_, b_, mc=mc: self.xv(mc, a_, b_), t, [("x", mc, t)])

        for t in range(NT):
            B1(t)
            LN(t)
            if t > 0:
                PW2(t - 1)
            LN_main(t)
        PW2(NT - 1)
        self.issue_next_fill(part="out", extra_writes=[("w2s", 0), ("w2s", 1)])
        self.after_conv = True

    def final_norm(self, p, raw=False):
        S = self.S
        self.phase_begin("BN" if self.after_conv else "B")
        NB = (NORM_OFF - 3456) // T
        ovb = lambda k: self.ar32(3456 + k * T, T)

        def tile(t, rstd, rk):
            t0, t1 = t * T, (t + 1) * T
            for c in range(DC):
                k = self.rotate("ov", NB)
                if raw:
                    self.A("dve", (lambda c=c, k=k: lambda e: e.tensor_copy(out=ovb(k), in_=self.xv(c, t0, t1)))(),
                           reads=[("x", c, t)], writes=[("ov", k)])
                else:
                    self.A("dve", (lambda c=c, k=k: lambda e: e.scalar_tensor_tensor(
                        out=ovb(k), in0=self.xv(c, t0, t1), scalar=self.g32col(DEPTH * 3 * DC + c), in1=rstd,
                        op0=ALU.mult, op1=ALU.mult))(),
                        reads=[("x", c, t), rk, "g32f"], writes=[("ov", k)], region="BN")
                self.A("sp", (lambda c=c, k=k: lambda e: e.dma_start(out=self.yT[p, c * 128:(c + 1) * 128, t0:t1], in_=ovb(k)))(),
                       reads=[("ov", k)], writes=[("y", p, t, c)], chan=self.ch_y)

        nxt = (None, None) if raw else self.rms_rstd(0, T)
        for t in range(NT):
            cur = nxt
            if not raw and t + 1 < NT:
                nxt = self.rms_rstd((t + 1) * T, (t + 2) * T)
            tile(t, *cur)


def make_cols(norm_g, final_g, pool_b, pool_scale, conv_b_pw1, conv_b_dw, conv_ln_g, conv_ln_b, conv_b_pw2, conv_w_dw):
    cols = np.zeros((128, NCOLS), np.float32)

    def put(name, idx0, vec):
        v = np.asarray(vec, np.float32).reshape(-1, 128).T
        o = COLS_OFF[name] + idx0
        cols[:, o:o + v.shape[1]] = v

    for l in range(DEPTH):
        for i in range(3):
            put("ng", (l * 3 + i) * DC, norm_g[l, i])
    put("fg", 0, final_g)
    for j in range(2):
        put("pool_b", j * DC, pool_b[j])
        put("pool_s", j * DC, pool_scale[j])
        put("b_pw1", j * 2 * DC, conv_b_pw1[j])
        put("b_dw", j * DC, conv_b_dw[j])
        put("ln_g", j * DC, conv_ln_g[j])
        put("ln_b", j * DC, conv_ln_b[j])
        put("b_pw2", j * DC, conv_b_pw2[j])
        for k in range(KCONV):
            put("w_dw", j * KCONV * DC + k * DC, conv_w_dw[j, k])
    return cols


def window_geometry():
    geo = []
    pos = np.arange(E)
    for w in range(BATCH * 4):
        b, jw = divmod(w, 4)
        start = jw * VALID - HALO
        tok = start + pos
        inside = (tok >= 0) & (tok < SEQ)
        mask = inside.astype(np.float32)
        cnt = np.ones((4, E), np.float32)
        for g, wd in enumerate(POOL_WINDOWS):
            lo = np.clip(tok - wd // 2, 0, SEQ)
            hi = np.clip(tok - wd // 2 + wd, 0, SEQ)
            c = (hi - lo).astype(np.float32)
            cnt[g] = 1.0 / np.where(inside & (c > 0), c, 1.0)
        geo.append((b, start, inside, mask, cnt))
    return geo


_NC_CACHE = {}


def get_nc(**kw):
    key = tuple(sorted(kw.items()))
    if key not in _NC_CACHE:
        _NC_CACHE[key] = Builder(**kw).build()
    return _NC_CACHE[key]


def make_in_maps(x, norm_g, ffn_w_in, ffn_w_out, pool_w, pool_b, pool_scale, conv_w_pw1, conv_b_pw1, conv_w_dw,
                 conv_b_dw, conv_ln_g, conv_ln_b, conv_w_pw2, conv_b_pw2, final_g, n_pass=NPASS):
    x = np.asarray(x, np.float32)
    cols = make_cols(np.asarray(norm_g), np.asarray(final_g), np.asarray(pool_b), np.asarray(pool_scale),
                     np.asarray(conv_b_pw1), np.asarray(conv_b_dw), np.asarray(conv_ln_g), np.asarray(conv_ln_b),
                     np.asarray(conv_b_pw2), np.asarray(conv_w_dw))
    geo = window_geometry()
    shared = {
        "cols": cols,
        "ident": np.eye(128, dtype=np.float32),
        "ffn_w_in": np.ascontiguousarray(ffn_w_in, dtype=np.float32),
        "ffn_w_out": np.ascontiguousarray(ffn_w_out, dtype=np.float32),
        "pool_w": np.ascontiguousarray(pool_w, dtype=np.float32),
        "conv_w_pw1": np.ascontiguousarray(conv_w_pw1, dtype=np.float32),
        "conv_w_pw2": np.ascontiguousarray(conv_w_pw2, dtype=np.float32),
    }
    in_maps = []
    for core in range(NCORES):
        xT = np.zeros((n_pass, D, E), np.float32)
        mask = np.zeros((n_pass, 128, E), np.float32)
        cnt = np.ones((n_pass, 4, 128, E), np.float32)
        for p in range(n_pass):
            w = core * NPASS + p
            b, start, inside, m, c = geo[w]
            lo = max(start, 0)
            hi = min(start + E, SEQ)
            xT[p][:, lo - start:hi - start] = x[b, lo:hi, :].T
            mask[p] = m[None, :]
            cnt[p] = c[:, None, :]
        m = dict(shared)
        m["xT"] = xT
        m["mask"] = mask
        m["cnt"] = cnt
        in_maps.append(m)
    return in_maps


def assemble(results, n_pass=NPASS):
    out = np.zeros((BATCH, SEQ, D), np.float32)
    for core in range(NCORES):
        yT = results[core]["yT"]
        for p in range(n_pass):
            w = core * NPASS + p
            b, jw = divmod(w, 4)
            out[b, jw * VALID:(jw + 1) * VALID, :] = yT[p][:, HALO:HALO + VALID].T
    return out


def kernel(**inputs):
    nc = get_nc()
    in_maps = make_in_maps(**inputs)
    res = run_bass_kernel_spmd(nc, in_maps, core_ids=list(range(NCORES)))
    return assemble(res.results)
```

```python
import contextlib
import numpy as np
import concourse.bass as bass
import concourse.mybir as mybir
from concourse.bass_utils import run_bass_kernel_spmd

F32 = mybir.dt.float32
BF16 = mybir.dt.bfloat16
ALU = mybir.AluOpType
AF = mybir.ActivationFunctionType

D = 1024
DC = 8
DFF = 2816
FC = 22
DEPTH = 4
KCONV = 31
SEQ = 8192
BATCH = 4
NCORES = 8
NPASS = 2
VALID = 2048
HALO = 56
T = 432
NT = 5
E = T * NT
MARG = 16
EM = E + 2 * MARG
RMS_EPS = 1e-6
LN_EPS = 1e-5
GROUPS = [(0, 6), (6, 12), (12, 17), (17, 22)]
GMAX = 6
WIN_SZ = DC * 2 * GMAX * 128
WSET_SZ = WIN_SZ + GMAX * D
POOL_WINDOWS = (2, 4, 8, 16)
HW = T + 16
DGSZ = KCONV * 128
W2S_OFF = 4 * DGSZ
ARENA = 7280
NORM_OFF = ARENA - 1344

CL, CR = (56, 72), (2088, 2104)
ML, MR = (0, 64), (2096, 2160)
ENGS = ("pe", "act", "dve", "pool", "sp")
SEM_SPAN = 20000


def _cols_layout():
    off = {}
    n = 0

    def add(name, k):
        nonlocal n
        off[name] = n
        n += k

    add("ng", DEPTH * 3 * DC)
    add("fg", DC)
    add("pool_b", 2 * DC)
    add("pool_s", 2 * DC)
    add("b_pw1", 2 * 2 * DC)
    add("b_dw", 2 * DC)
    add("ln_g", 2 * DC)
    add("ln_b", 2 * DC)
    add("b_pw2", 2 * DC)
    add("w_dw", 2 * KCONV * DC)
    return off, n


COLS_OFF, NCOLS = _cols_layout()


class Chan:
    def __init__(self, sem):
        self.sem = sem
        self.count = 0


class Op:
    __slots__ = ("eng", "fn", "deps", "need", "chan", "sigval", "sem", "is_dma", "short")

    def __init__(self, eng, fn, chan):
        self.eng = eng
        self.fn = fn
        self.deps = []
        self.need = False
        self.chan = chan
        self.sigval = None
        self.sem = None
        self.is_dma = chan is not None
        self.short = False


class Sched:
    def __init__(self, same_engine_sync=True):
        self.streams = {e: [] for e in ENGS}
        self.lastw = {}
        self.rd_eng = {}
        self.rd_dma = {}
        self.same_engine_sync = same_engine_sync

    def add(self, eng, fn, reads=(), writes=(), chan=None, short=False):
        op = Op(eng, fn, chan)
        op.short = short
        deps = {}

        def dep(o):
            if o is None or o is op:
                return
            if o.is_dma:
                deps[id(o)] = (o, 16 * o.chan.count)
                return
            if o.eng == eng and not op.is_dma:
                if eng == "pe" or not (self.same_engine_sync or o.short or short):
                    return
            deps[id(o)] = (o, None)

        for k in reads:
            dep(self.lastw.get(k))
        for k in writes:
            dep(self.lastw.get(k))
            for o in self.rd_eng.get(k, {}).values():
                dep(o)
            for o in self.rd_dma.get(k, ()):
                dep(o)
        for k in writes:
            self.lastw[k] = op
            self.rd_eng[k] = {}
            self.rd_dma[k] = []
        for k in reads:
            if op.is_dma:
                self.rd_dma.setdefault(k, []).append(op)
            else:
                self.rd_eng.setdefault(k, {})[eng] = op
        if chan is not None:
            chan.count += 1
            op.sigval = 16 * chan.count
            op.sem = chan.sem
        op.deps = list(deps.values())
        self.streams[eng].append(op)
        return op

    def emit(self, block, eng_sems):
        for e in ENGS:
            for op in self.streams[e]:
                for d, _ in op.deps:
                    d.need = True
        for e in ENGS:
            cnt = 0
            for op in self.streams[e]:
                if op.is_dma:
                    continue
                if op.need:
                    si = cnt // SEM_SPAN
                    cnt += 1
                    op.sem = eng_sems[e][si]
                    op.sigval = cnt - si * SEM_SPAN
        self.stats = {}

        def run_stream(e):
            def body(eng):
                waited = {}
                nw = 0
                for op in self.streams[e]:
                    for d, v in op.deps:
                        val = d.sigval if v is None else v
                        key = id(d.sem)
                        if waited.get(key, 0) >= val:
                            continue
                        waited[key] = val
                        eng.wait_ge(d.sem, val)
                        nw += 1
                    if op.fn is None:
                        continue
                    ins = op.fn(eng)
                    if op.is_dma:
                        ins.then_inc(op.sem, 16)
                    elif op.need:
                        ins.then_inc(op.sem, 1)
                self.stats[e] = (len(self.streams[e]), nw)
            return body

        block.tensor(run_stream("pe"))
        block.scalar(run_stream("act"))
        block.vector(run_stream("dve"))
        block.gpsimd(run_stream("pool"))
        block.sync(run_stream("sp"))


class Builder:
    def __init__(self, n_pass=NPASS, layers=DEPTH, same_engine_sync=True, stop_after=None):
        self.n_pass = n_pass
        self.layers = layers
        self.stop_after = stop_after
        self.S = Sched(same_engine_sync)
        self.nc = bass.Bass("TRN2", target_bir_lowering=False)
        self.rot = {}
        self.fill_list = []
        self.fill_pos = 0
        self.fill_set = {}
        self.wtoggle = 0
        self.pending_fill = None
        self.cur_region = "AB"
        self.after_conv = False

    def dram_in(self, name, shape, dt=F32):
        return self.nc.dram_tensor(name, list(shape), dt, kind="ExternalInput").ap()

    def rotate(self, name, n):
        i = self.rot.get(name, 0)
        self.rot[name] = (i + 1) % n
        return i

    def col(self, name, idx):
        c = COLS_OFF[name] + idx
        return self.cols[:, c:c + 1]

    def build(self):
        nc = self.nc
        npass = self.n_pass
        self.xT = self.dram_in("xT", [npass, D, E])
        self.maskd = self.dram_in("mask", [npass, 128, E])
        self.cntd = self.dram_in("cnt", [npass, 4, 128, E])
        self.colsd = self.dram_in("cols", [128, NCOLS])
        self.identd = self.dram_in("ident", [128, 128])
        self.w_in = self.dram_in("ffn_w_in", [DEPTH, 2, D, 2 * DFF])
        self.w_out = self.dram_in("ffn_w_out", [DEPTH, 2, DFF, D])
        self.pool_w = self.dram_in("pool_w", [2, 4, 256, 256])
        self.w_pw1 = self.dram_in("conv_w_pw1", [2, D, 2 * D])
        self.w_pw2 = self.dram_in("conv_w_pw2", [2, D, D])
        self.yT = nc.dram_tensor("yT", [npass, D, E], F32, kind="ExternalOutput").ap()

        with contextlib.ExitStack() as st:
            def sb(name, shape, dt):
                return st.enter_context(nc.sbuf_tensor(name, list(shape), dt))

            self.x = sb("x_sb", [128, DC * E], F32)
            self.xn = sb("xn_sb", [128, DC * EM], BF16)
            self.wset = [sb("wset0", [128, WSET_SZ], BF16), sb("wset1", [128, WSET_SZ], BF16)]
            self.cols = sb("cols_sb", [128, NCOLS], F32)
            self.g32 = sb("g32_sb", [128, (DEPTH * 3 + 1) * DC + 2 * DC], F32)
            self.bndc = sb("bndc_sb", [128, 2 * 4 * 16], F32)
            self.bndm = sb("bndm_sb", [128, 2 * 64], F32)
            self.ones = sb("ones_sb", [128, 128], BF16)
            self.ident = sb("ident_sb", [128, 128], BF16)
            self.scr = sb("scr_sb", [128, 2], F32)
            self.epsc = sb("eps_sb", [128, 2], F32)
            self.arena = sb("arena", [128, ARENA], F32)
            self.ps = [st.enter_context(nc.psum_tensor(f"ps{i}", [128, 512], F32)) for i in range(8)]
            sems = {e: [st.enter_context(nc.semaphore(f"s_{e}{i}")) for i in range(4)] for e in ENGS}

            def ch(name):
                return Chan(st.enter_context(nc.semaphore(name)))

            self.ch_x = [ch(f"c_x{t}") for t in range(NT)]
            self.ch_w = [ch("c_w0"), ch("c_w1")]
            self.ch_m = [ch("c_m0"), ch("c_m1")]
            self.ch_c = ch("c_c")
            self.ch_y = ch("c_y")
            self.ch_k = ch("c_k")

            self.program()

            with nc.Block() as block:
                self.S.emit(block, sems)
        return nc

    def xv(self, c, a, b):
        return self.x[:, c * E + a: c * E + b]

    def xnv(self, c, a, b):
        return self.xn[:, c * EM + MARG + a: c * EM + MARG + b]

    def ar32(self, off, n):
        return self.arena[:, off:off + n]

    def ar16(self, off32, n16):
        v = self.arena[:, off32:off32 + (n16 + 1) // 2].bitcast(BF16)
        return v[:, 0:n16]

    @staticmethod
    def tkeys(name, c, a, b):
        t0 = max(a, 0) // T
        t1 = (min(b, E) - 1) // T
        return [(name, c, t) for t in range(t0, t1 + 1)]

    def A(self, eng, fn, reads=(), writes=(), chan=None, short=False, region=None):
        region = region or self.cur_region
        return self.S.add(eng, fn, list(reads) + ["arena" + r for r in region], writes, chan, short)

    def mask_fix(self, view_fn, t, keys):
        if t == 0:
            side, (a, b) = 0, ML
        elif t == NT - 1:
            side, (a, b) = 1, MR
        else:
            return
        v = view_fn(a, b)
        m = self.bndm[:, side * 64:(side + 1) * 64]
        self.S.add("dve", lambda e: e.tensor_tensor(out=v, in0=v, in1=m, op=ALU.mult),
                   reads=list(keys) + [("bndm", side)], writes=list(keys), short=True)

    def phase_begin(self, regions="ABN"):
        self.cur_region = regions
        self.S.add("dve", lambda e: e.memset(self.scr[:, 0:1], 0.0), writes=["arena" + r for r in regions], short=True)

    def plan_fills(self):
        fl = []
        for p in range(self.n_pass):
            for l in range(self.layers):
                for g in range(len(GROUPS)):
                    fl.append(("ffn", p, l, 0, g))
                if self.stop_after == (l, 0):
                    break
                if l % 2 == 0:
                    fl.append(("poolw", p, l))
                else:
                    fl.append(("w1", p, l))
                    fl.append(("w2", p, l))
                if self.stop_after == (l, 1):
                    break
                for g in range(len(GROUPS)):
                    fl.append(("ffn", p, l, 1, g))
                if self.stop_after == (l, 2):
                    break
        self.fill_list = fl

    def issue_next_fill(self, part=None, extra_writes=()):
        if part == "out":
            if self.pending_fill is None:
                return
            f, s = self.pending_fill
            self.pending_fill = None
        else:
            if self.fill_pos >= len(self.fill_list):
                return
            f = self.fill_list[self.fill_pos]
            self.fill_pos += 1
            s = self.wtoggle
            self.wtoggle ^= 1
            self.fill_set[f] = s
            if part == "in":
                assert f[0] == "ffn"
                self.pending_fill = (f, s)
        S = self.S
        ws = self.wset[s]
        chn = self.ch_w[s]
        kind = f[0]
        first = [part != "out"]

        xw = [list(extra_writes)]

        def dma(dst, src):
            wr = ([("wset", s)] if first[0] else []) + xw[0]
            first[0] = False
            xw[0] = []
            S.add("pool", lambda e: e.dma_start(out=dst, in_=src), writes=wr, chan=chn)

        if kind == "ffn":
            _, p, l, i, g = f
            f0, f1 = GROUPS[g]
            n = f1 - f0
            if part != "out":
                dstw = ws[:, 0:WIN_SZ].rearrange("p (kc h j) -> p kc h j", kc=DC, h=2, j=GMAX * 128)
                for half in range(2):
                    src = self.w_in[l, i, :, half * DFF + f0 * 128: half * DFF + f1 * 128].rearrange("(kc p) f -> p kc f", p=128)
                    dst = dstw[:, :, half, 0:n * 128]
                    dma(dst, src)
            if part != "in":
                src = self.w_out[l, i, f0 * 128:f1 * 128, :].rearrange("(n p) d -> p n d", p=128)
                dst = ws[:, WIN_SZ:WIN_SZ + n * D].rearrange("p (n d) -> p n d", d=D)
                dma(dst, src)
        elif kind == "poolw":
            _, p, l = f
            j = l // 2
            src = self.pool_w[j].rearrange("g (kc p) m -> p g kc m", p=128)
            dst = ws[:, 10752:10752 + 2048].rearrange("p (g kc m) -> p g kc m", g=4, kc=2)
            dma(dst, src)
        elif kind == "w1":
            _, p, l = f
            j = l // 2
            src = self.w_pw1[j].rearrange("(kc p) m -> p kc m", p=128)
            dst = ws[:, 0:DC * 2 * D].rearrange("p (kc m) -> p kc m", kc=DC)
            dma(dst, src)
        elif kind == "w2":
            _, p, l = f
            j = l // 2
            src = self.w_pw2[j][2 * 128:4 * 128, :].rearrange("(kc p) m -> p kc m", p=128)
            dst = ws[:, W2S_OFF:W2S_OFF + 2 * D].rearrange("p (kc m) -> p kc m", kc=2)
            dma(dst, src)

    def program(self):
        S = self.S
        self.plan_fills()
        S.add("sp", lambda e: e.dma_start(out=self.cols[:], in_=self.colsd[:, :]), writes=["cols"], chan=self.ch_k)
        self.issue_next_fill()
        S.add("dve", lambda e: e.memset(self.ones[:], 1.0), writes=["ones"], short=True)
        S.add("dve", lambda e: e.memset(self.epsc[:, 0:1], float(RMS_EPS)), writes=["epsc"], short=True)
        S.add("dve", lambda e: e.memset(self.epsc[:, 1:2], float(LN_EPS)), reads=["epsc"], writes=["epsc"], short=True)
        S.add("dve", lambda e: e.memset(self.xn[:], 0.0),
              writes=[("xn", c, t) for c in range(DC) for t in range(NT)])
        ng0 = COLS_OFF["ng"]
        S.add("dve", lambda e: e.tensor_scalar(out=self.g32[:, 0:DEPTH * 3 * DC], in0=self.cols[:, ng0:ng0 + DEPTH * 3 * DC],
                                               scalar1=1.0, scalar2=None, op0=ALU.mult), reads=["cols"], writes=["g32"], short=True)
        fg0 = COLS_OFF["fg"]
        S.add("dve", lambda e: e.tensor_scalar(out=self.g32[:, DEPTH * 3 * DC:(DEPTH * 3 + 1) * DC], in0=self.cols[:, fg0:fg0 + DC],
                                               scalar1=1.0, scalar2=None, op0=ALU.mult), reads=["cols"], writes=["g32f"], short=True)
        S.add("pool", lambda e: e.dma_start(out=self.ident[:], in_=self.identd[:, :]), writes=["ident"], chan=self.ch_m[0])
        BS0 = (DEPTH * 3 + 1) * DC
        pb0, ps0 = COLS_OFF["pool_b"], COLS_OFF["pool_s"]
        S.add("dve", lambda e: e.tensor_tensor(out=self.g32[:, BS0:BS0 + 2 * DC], in0=self.cols[:, pb0:pb0 + 2 * DC],
                                               in1=self.cols[:, ps0:ps0 + 2 * DC], op=ALU.mult), reads=["cols"], writes=["bs"], short=True)

        for p in range(self.n_pass):
            self.p = p
            for side, (ca_, cb_) in enumerate((CL, CR)):
                dst = self.bndc[:, side * 64:(side + 1) * 64].rearrange("p (g n) -> p g n", g=4)
                S.add("sp", (lambda dst=dst, ca_=ca_, cb_=cb_, p=p: lambda e: e.dma_start(
                    out=dst, in_=self.cntd[p, :, :, ca_:cb_].rearrange("g p n -> p g n")))(),
                    writes=[("bndc", side)], chan=self.ch_c)
            for side, (ma_, mb_) in enumerate((ML, MR)):
                S.add("sp", (lambda side=side, ma_=ma_, mb_=mb_, p=p: lambda e: e.dma_start(
                    out=self.bndm[:, side * 64:(side + 1) * 64], in_=self.maskd[p, :, ma_:mb_]))(),
                    writes=[("bndm", side)], chan=self.ch_c)
            for t in range(NT):
                for c in range(DC):
                    S.add("sp", (lambda c=c, p=p, t=t: lambda e: e.dma_start(
                        out=self.xv(c, t * T, (t + 1) * T), in_=self.xT[p, c * 128:(c + 1) * 128, t * T:(t + 1) * T]))(),
                        writes=[("x", c, t)], chan=self.ch_x[t])
            done = False
            for l in range(self.layers):
                self.ffn(p, l, 0)
                if self.stop_after == (l, 0):
                    done = True
                    break
                hook = None
                if l % 2 == 0:
                    hook = self.pool_mixer(p, l, defer=False)
                else:
                    self.conv_mixer(p, l)
                if self.stop_after == (l, 1):
                    done = True
                    break
                self.ffn(p, l, 1, hook=hook)
                if self.stop_after == (l, 2):
                    done = True
                    break
            self.final_norm(p, raw=done)
        S.add("sp", None, reads=[("y", p, t, c) for p in range(self.n_pass) for t in range(NT) for c in range(DC)])

    def rms_rstd(self, a, b):
        n = b - a
        i = self.rotate("rstd", 2)
        rstd = self.ar32(NORM_OFF + i * 448, n)
        for c in range(DC):
            j = self.rotate("sq", 2)
            sq = self.ar16(NORM_OFF + 896 + j * 224, n)
            self.A("act", (lambda sq=sq, c=c: lambda e: e.activation(out=sq, in_=self.xv(c, a, b), func=AF.Square))(),
                   reads=self.tkeys("x", c, a, b), writes=[("sq", j)], region="N")
            self.A("pe", (lambda sq=sq, c=c: lambda e: e.matmul(self.ps[7][:, 0:n], lhsT=self.ones[:], rhs=sq,
                                                              start=(c == 0), stop=(c == DC - 1)))(),
                   reads=[("sq", j), "ones"], writes=[("ps", 7)], region="N")
        self.A("act", lambda e: e.activation(out=rstd, in_=self.ps[7][:, 0:n], func=AF.Sqrt, bias=self.epsc[:, 0:1], scale=1.0 / D),
               reads=[("ps", 7), "epsc"], writes=[("rstd", i)], region="N")
        self.A("dve", lambda e: e.reciprocal(out=rstd, in_=rstd), reads=[("rstd", i)], writes=[("rstd", i)], region="N")
        return rstd, ("rstd", i)

    def g32col(self, idx):
        return self.g32[:, idx:idx + 1]

    def ffn_norm_tile(self, gbase, t, t0, t1):
        rstd, rk = self.rms_rstd(t0, t1)
        for c in range(DC):
            self.A("dve", (lambda c=c: lambda e: e.scalar_tensor_tensor(
                out=self.xnv(c, t0, t1), in0=self.xv(c, t0, t1), scalar=self.g32col(gbase + c), in1=rstd,
                op0=ALU.mult, op1=ALU.mult))(),
                reads=[("x", c, t), rk, "g32"], writes=[("xn", c, t)], region="N")

    def ffn(self, p, l, i, hook=None):
        if hook is None:
            self.phase_begin("AN" if self.after_conv else "A")
            self.after_conv = False
        gbase = (l * 3 + (0 if i == 0 else 2)) * DC
        steps = [(g, t) for g in range(len(GROUPS)) for t in range(NT)]
        full = (self.layers == DEPTH and self.stop_after is None)
        m = ((46, 38, 23, 15)[l] if i == 0 else (38, 23, 15, 0)[l]) if full else HALO
        lo_trim = (HALO - m) // 4 * 4
        hi_end = min(T, -(-(E - HALO + m - (NT - 1) * T) // 4) * 4)

        def trange(t):
            return t * T + (lo_trim if t == 0 else 0), (t * T + hi_end) if t == NT - 1 else (t + 1) * T

        def GT(gs, fi):
            return self.ar16(gs * (GMAX * T // 2) + fi * (T // 2), T)

        def SG(j):
            return self.ar32(2 * GMAX * T // 2 + j * T, T)

        def do_H(k):
            g, t = steps[k]
            if g == 0 and t + 1 < NT:
                if hook is not None:
                    hook(t + 1)
                    if t + 1 == NT - 1:
                        self.issue_next_fill()
                self.ffn_norm_tile(gbase, t + 1, *trange(t + 1))
            s = self.fill_set[("ffn", p, l, i, g)]
            f0, f1 = GROUPS[g]
            n = f1 - f0
            t0, t1 = trange(t)
            nt = t1 - t0
            gs = k % 2
            win = self.wset[s][:, 0:WIN_SZ].rearrange("p (kc h j) -> p kc h j", kc=DC, h=2, j=GMAX * 128)
            for fi in range(n):
                r = self.rotate("hb", 2)
                bg, bu = r, 2 + r
                for half, bank in ((0, bg), (1, bu)):
                    for kc in range(DC):
                        self.S.add("pe", (lambda kc=kc, half=half, bank=bank, fi=fi: lambda e: e.matmul(
                            self.ps[bank][:, 0:nt], lhsT=win[:, kc, half, fi * 128:(fi + 1) * 128],
                            rhs=self.xnv(kc, t0, t1), start=(kc == 0), stop=(kc == DC - 1)))(),
                            reads=[("wset", s), ("xn", kc, t)], writes=[("ps", bank)])
                j = self.rotate("sg", 2)
                sg = SG(j)
                self.A("act", (lambda sg=sg, bg=bg: lambda e: e.activation(out=sg[:, 0:nt], in_=self.ps[bg][:, 0:nt], func=AF.Silu))(),
                       reads=[("ps", bg)], writes=[("sg", j)])
                gt = GT(gs, fi)
                self.A("dve", (lambda gt=gt, sg=sg, bu=bu: lambda e: e.tensor_tensor(
                    out=gt[:, 0:nt], in0=self.ps[bu][:, 0:nt], in1=sg[:, 0:nt], op=ALU.mult))(),
                    reads=[("ps", bu), ("sg", j)], writes=[("gt", gs, fi)])

        def do_OUT(k):
            g, t = steps[k]
            s = self.fill_set[("ffn", p, l, i, g)]
            f0, f1 = GROUPS[g]
            n = f1 - f0
            t0, t1 = trange(t)
            nt = t1 - t0
            gs = k % 2
            wout = self.wset[s][:, WIN_SZ:WIN_SZ + GMAX * D].rearrange("p (n d) -> p n d", d=D)
            for dc in range(DC):
                bo = 4 + self.rotate("ob", 3)
                for fi in range(n):
                    self.A("pe", (lambda fi=fi, dc=dc, bo=bo: lambda e: e.matmul(
                        self.ps[bo][:, 0:nt], lhsT=wout[:, fi, dc * 128:(dc + 1) * 128], rhs=GT(gs, fi)[:, 0:nt],
                        start=(fi == 0), stop=(fi == n - 1)))(),
                        reads=[("wset", s), ("gt", gs, fi)], writes=[("ps", bo)])
                self.S.add("dve", (lambda dc=dc, bo=bo: lambda e: e.scalar_tensor_tensor(
                    out=self.xv(dc, t0, t1), in0=self.ps[bo][:, 0:nt], scalar=0.5, in1=self.xv(dc, t0, t1),
                    op0=ALU.mult, op1=ALU.add))(),
                    reads=[("ps", bo), ("x", dc, t)], writes=[("x", dc, t)])

        if hook is None:
            self.issue_next_fill()
        else:
            hook(0)
        self.ffn_norm_tile(gbase, 0, *trange(0))
        do_H(0)
        for k in range(len(steps)):
            if k + 1 < len(steps):
                do_H(k + 1)
            do_OUT(k)
            if steps[k][1] == NT - 1 and k + 1 < len(steps):
                self.issue_next_fill()

    def load_mask(self, p, t):
        raise NotImplementedError

    def pool_mixer(self, p, l, defer=True):
        S = self.S
        self.phase_begin("B")
        j = l // 2
        s = self.fill_set[("poolw", p, l)]
        self.issue_next_fill()
        wsf = self.wset[s][:, :].bitcast(F32)
        WS = ("wset", s)

        def hv(c, a, b):
            return wsf[:, c * HW + a: c * HW + b]

        def stmp(k, a, b):
            return wsf[:, 8 * HW + k * HW + a: 8 * HW + k * HW + b]

        pw = self.wset[s][:, 10752:10752 + 2048].rearrange("p (g kc m) -> p g kc m", g=4, kc=2)
        pooled = lambda c: self.wset[s][:, 12800 + c * T: 12800 + (c + 1) * T]
        ytmp = lambda k: self.ar32(3456 + k * T, T)
        tmpf = lambda k: self.ar32(4320 + 16 * k, 16)
        BS0 = (DEPTH * 3 + 1) * DC

        def tile(t, rstd, rk):
            t0, t1 = t * T, (t + 1) * T
            ca, cb = t0, min(t1 + 8, E)
            lo = 8
            n = cb - ca
            for c in range(DC):
                if t == 0:
                    S.add("dve", (lambda c=c: lambda e: e.memset(hv(c, 0, 8), 0.0))(), reads=[WS], writes=[("h", c)], short=True)
                else:
                    S.add("dve", (lambda c=c: lambda e: e.tensor_copy(out=hv(c, 0, 8), in_=hv(c, T, T + 8)))(),
                          reads=[WS, ("h", c)], writes=[("h", c)], short=True)
            for c in range(DC):
                S.add("dve", (lambda c=c: lambda e: e.scalar_tensor_tensor(
                    out=hv(c, lo, lo + n), in0=self.xv(c, ca, cb), scalar=self.g32col((l * 3 + 1) * DC + c), in1=rstd,
                    op0=ALU.mult, op1=ALU.mult))(),
                    reads=self.tkeys("x", c, ca, cb) + [rk, "g32", WS, "arenaB", "arenaN", ("h", c)], writes=[("h", c)])
            if lo + n < HW:
                for c in range(DC):
                    S.add("dve", (lambda c=c: lambda e: e.memset(hv(c, lo + n, HW), 0.0))(), reads=[WS], writes=[("h", c)], short=True)
            lohi = [(1, HW, 1, 0), (2, HW - 1, 1, 1), (4, HW - 3, 2, 2), (8, HW - 7, 4, 4)]
            for g in range(4):
                pair = (2 * g, 2 * g + 1)
                levels = g + 1
                kbs = {pair[0]: 0, pair[1]: 2}
                srcs = {c: (lambda c: (lambda a_, b_: hv(c, a_, b_)))(c) for c in pair}
                srcks = {c: ("h", c) for c in pair}
                for lv in range(levels):
                    ja, jb, dl, dr = lohi[lv]
                    for c in pair:
                        dk = kbs[c] + (lv % 2)
                        dst = stmp(dk, ja, jb)
                        in0 = srcs[c](ja - dl, jb - dl)
                        in1 = srcs[c](ja + dr, jb + dr)
                        S.add("dve", (lambda dst=dst, in0=in0, in1=in1: lambda e: e.tensor_tensor(out=dst, in0=in0, in1=in1, op=ALU.add))(),
                              reads=[srcks[c], WS], writes=[("stmp", dk)])
                        srcs[c] = (lambda dk: (lambda a_, b_: stmp(dk, a_, b_)))(dk)
                        srcks[c] = ("stmp", dk)
                for c in pair:
                    fsrc = srcs[c](8, 8 + T)
                    S.add("dve", (lambda fsrc=fsrc, c=c, g=g: lambda e: e.scalar_tensor_tensor(
                        out=pooled(c), in0=fsrc, scalar=1.0 / POOL_WINDOWS[g], in1=hv(c, 8, 8 + T),
                        op0=ALU.mult, op1=ALU.subtract))(),
                        reads=[srcks[c], ("h", c), WS, "arenaB"], writes=[("pooled", c)])
                if t in (0, NT - 1):
                    side, (wa, wb) = (0, CL) if t == 0 else (1, CR)
                    la, lb = wa - t0, wb - t0
                    icv = self.bndc[:, side * 64 + g * 16: side * 64 + (g + 1) * 16]
                    for k, c in enumerate(pair):
                        fs2 = srcs[c](8 + la, 8 + lb)
                        tf = tmpf(k)
                        S.add("dve", (lambda fs2=fs2, icv=icv, tf=tf: lambda e: e.tensor_tensor(out=tf, in0=fs2, in1=icv, op=ALU.mult))(),
                              reads=[srcks[c], ("bndc", side), WS, "arenaB"], writes=[("tmpf", k)], short=True)
                    for k, c in enumerate(pair):
                        tf = tmpf(k)
                        S.add("dve", (lambda c=c, la=la, lb=lb, tf=tf: lambda e: e.tensor_tensor(
                            out=pooled(c)[:, la:lb], in0=tf, in1=hv(c, 8 + la, 8 + lb), op=ALU.subtract))(),
                            reads=[("tmpf", k), ("h", c), WS, "arenaB", ("pooled", c)], writes=[("pooled", c)], short=True)
            for g in range(4):
                for mc in range(2):
                    c = 2 * g + mc
                    bo = 4 + self.rotate("ob", 3)
                    for kc in range(2):
                        S.add("pe", (lambda g=g, mc=mc, kc=kc, bo=bo: lambda e: e.matmul(
                            self.ps[bo][:, 0:T], lhsT=pw[:, g, kc, mc * 128:(mc + 1) * 128], rhs=pooled(2 * g + kc),
                            start=(kc == 0), stop=(kc == 1)))(),
                            reads=[WS, ("pooled", 2 * g + kc), "arenaB"], writes=[("ps", bo)])
                    yk = self.rotate("ytmp", 2)
                    self.A("act", (lambda c=c, bo=bo, yk=yk: lambda e: e.activation(
                        out=ytmp(yk), in_=self.ps[bo][:, 0:T], func=AF.Identity,
                        bias=self.g32[:, BS0 + j * DC + c: BS0 + j * DC + c + 1], scale=self.col("pool_s", j * DC + c)))(),
                        reads=[("ps", bo), "bs", "cols"], writes=[("ytmp", yk)])
                    self.A("dve", (lambda c=c, yk=yk: lambda e: e.tensor_tensor(
                        out=self.xv(c, t0, t1), in0=self.xv(c, t0, t1), in1=ytmp(yk), op=ALU.add))(),
                        reads=[("ytmp", yk), ("x", c, t)], writes=[("x", c, t)])
                    self.mask_fix(lambda a_, b_, c=c: self.xv(c, a_, b_), t, [("x", c, t)])

        rms = lambda t: self.rms_rstd(t * T, min((t + 1) * T + 8, E))
        nxt = rms(0)
        for t in range(NT):
            cur = nxt
            if t + 1 < NT:
                nxt = rms(t + 1)
            tile(t, *cur)
        return None

    def conv_mixer(self, p, l):
        S = self.S
        j = l // 2
        self.phase_begin("AB")
        sA = self.fill_set[("w1", p, l)]
        self.issue_next_fill()
        for c in range(DC):
            S.add("dve", (lambda c=c: lambda e: e.memset(self.xnv(c, -MARG, 0), 0.0))(), writes=[("R", c, 0)], short=True)
        WA = ("wset", sA)
        w1 = self.wset[sA][:, 0:DC * 2 * D].rearrange("p (kc m) -> p kc m", kc=DC)
        xnt = lambda k, c: self.ar16(k * (DC * T // 2) + c * (T // 2), T)
        sgm = lambda k: self.ar32(3456 + k * T, T)
        glv = lambda k: self.ar32(4320 + k * T, T)
        maskA = lambda k: self.ar16(5184 + k * (T // 2), T)
        def normA(t):
            t0, t1 = t * T, (t + 1) * T
            mk = 0
            rstd, rk = self.rms_rstd(t0, t1)
            xk = self.rotate("xnt", 2)
            for c in range(DC):
                self.A("dve", (lambda c=c, xk=xk: lambda e: e.scalar_tensor_tensor(
                    out=xnt(xk, c), in0=self.xv(c, t0, t1), scalar=self.g32col((l * 3 + 1) * DC + c), in1=rstd,
                    op0=ALU.mult, op1=ALU.mult))(),
                    reads=[("x", c, t), rk, "g32"], writes=[("xnt", xk, c)], region="ABN")
            return xk, mk

        def tileA(t, xk, mk):
            t0, t1 = t * T, (t + 1) * T
            for mc in range(DC):
                r = self.rotate("hb", 2)
                ba, bg = r, 2 + r
                for half, bank in ((0, ba), (1, bg)):
                    for kc in range(DC):
                        self.A("pe", (lambda kc=kc, half=half, bank=bank, mc=mc, xk=xk: lambda e: e.matmul(
                            self.ps[bank][:, 0:T], lhsT=w1[:, kc, half * D + mc * 128: half * D + (mc + 1) * 128],
                            rhs=xnt(xk, kc), start=(kc == 0), stop=(kc == DC - 1)))(),
                            reads=[WA, ("xnt", xk, kc)], writes=[("ps", bank)])
                sk = self.rotate("sgm", 2)
                self.A("act", (lambda sk=sk, bg=bg, mc=mc: lambda e: e.activation(
                    out=sgm(sk), in_=self.ps[bg][:, 0:T], func=AF.Sigmoid, bias=self.col("b_pw1", j * 2 * DC + DC + mc)))(),
                    reads=[("ps", bg), "cols"], writes=[("sgm", sk)])
                self.A("dve", (lambda sk=sk, ba=ba, mc=mc: lambda e: e.scalar_tensor_tensor(
                    out=self.xnv(mc, t0, t1), in0=self.ps[ba][:, 0:T], scalar=self.col("b_pw1", j * 2 * DC + mc), in1=sgm(sk),
                    op0=ALU.add, op1=ALU.mult))(),
                    reads=[("ps", ba), ("sgm", sk), "cols"], writes=[("xn", mc, t)])
                self.mask_fix(lambda a_, b_, mc=mc: self.xnv(mc, a_, b_), t, [("xn", mc, t)])

        sB = self.fill_set[("w2", p, l)]
        WB = ("wset", sB)
        dgv = lambda c, tap: (self.wset[sA] if c < 4 else self.wset[sB])[:, (c % 4) * DGSZ + tap * 128:(c % 4) * DGSZ + (tap + 1) * 128]

        def build_diags(c, first_in_set=False):
            for tap in range(KCONV):
                wr = [("dgr", c, tap)]
                rd = ["ident", "cols"]
                if c < 4:
                    if first_in_set and tap == 0:
                        wr.append(WA)
                    else:
                        rd.append(WA)
                else:
                    rd.append(WB)
                S.add("dve", (lambda tap=tap: lambda e: e.tensor_scalar(
                    out=dgv(c, tap), in0=self.ident[:], scalar1=self.col("w_dw", j * KCONV * DC + tap * DC + c), scalar2=None,
                    op0=ALU.mult))(), reads=rd, writes=wr)

        nxt = normA(0)
        for t in range(NT):
            cur = nxt
            if t + 1 < NT:
                nxt = normA(t + 1)
            tileA(t, *cur)
            if t < 4:
                build_diags(4 + t)

        self.phase_begin()
        co = lambda c: self.ar32(c * T, T)
        co16 = self.ar16(3456, T)
        sq16 = self.ar16(3456 + T // 2, T)
        w2a = lambda kc: self.ar16(3888 + (kc - 4) * (D // 2), D)
        mean = self.ar32(NORM_OFF, T)
        var = self.ar32(NORM_OFF + T, T)
        rsd = self.ar32(NORM_OFF + 2 * T, T)
        zv = lambda c, t: self.xnv(c, t * T - 16, (t + 1) * T - 16)

        def w2v(kc, mc):
            if kc < 2:
                return self.wset[sA][:, W2S_OFF + kc * D + mc * 128: W2S_OFF + kc * D + (mc + 1) * 128]
            if kc < 4:
                return self.wset[sB][:, W2S_OFF + (kc - 2) * D + mc * 128: W2S_OFF + (kc - 2) * D + (mc + 1) * 128]
            return w2a(kc)[:, mc * 128:(mc + 1) * 128]

        build_diags(0, first_in_set=True)
        srcA = self.w_pw2[j][0:2 * 128, :].rearrange("(kc p) m -> p kc m", p=128)
        dstA = self.wset[sA][:, W2S_OFF:W2S_OFF + 2 * D].rearrange("p (kc m) -> p kc m", kc=2)
        S.add("pool", lambda e: e.dma_start(out=dstA, in_=srcA), reads=[WA], writes=[("w2s", 0), ("w2s", 1)], chan=self.ch_w[sA])
        for kc in range(4, DC):
            self.A("pool", (lambda kc=kc: lambda e: e.dma_start(out=w2a(kc), in_=self.w_pw2[j][kc * 128:(kc + 1) * 128, :]))(),
                   writes=[("w2s", kc)], chan=self.ch_m[1])
        for c in range(1, 4):
            build_diags(c)

        def B1(t):
            t0, t1 = t * T, (t + 1) * T

            def stats(c):
                self.A("pe", (lambda c=c: lambda e: e.matmul(self.ps[6][:, 0:T], lhsT=self.ones[:], rhs=co16,
                                                            start=(c == 0), stop=(c == DC - 1)))(),
                       reads=["co16", "ones"], writes=[("ps", 6)])
                self.A("pe", (lambda c=c: lambda e: e.matmul(self.ps[7][:, 0:T], lhsT=self.ones[:], rhs=sq16,
                                                            start=(c == 0), stop=(c == DC - 1)))(),
                       reads=["sq16", "ones"], writes=[("ps", 7)])

            for c in range(DC):
                bo = 4 + self.rotate("cvb", 2)
                rkeys = [("R", c, t), ("R", c, t + 1)] + self.tkeys("xn", c, t0 - 15, t1 + 15)
                if c >= 4:
                    rkeys.append(WB)
                for tap in range(KCONV):
                    S.add("pe", (lambda tap=tap, c=c, bo=bo: lambda e: e.matmul(
                        self.ps[bo][:, 0:T], lhsT=dgv(c, tap), rhs=self.xnv(c, t0 + tap - 15, t1 + tap - 15),
                        start=(tap == 0), stop=(tap == KCONV - 1)))(),
                        reads=rkeys + [("dgr", c, tap)], writes=[("ps", bo)])
                if c > 0:
                    stats(c - 1)
                bcol = self.col("b_dw", j * DC + c)
                self.A("act", (lambda c=c, bo=bo, bcol=bcol: lambda e: e.activation(
                    out=co(c), in_=self.ps[bo][:, 0:T], func=AF.Identity, bias=bcol))(),
                    reads=[("ps", bo), "cols"], writes=[("co", c)])
                self.A("act", (lambda bo=bo, bcol=bcol: lambda e: e.activation(
                    out=co16, in_=self.ps[bo][:, 0:T], func=AF.Identity, bias=bcol))(),
                    reads=[("ps", bo), "cols"], writes=["co16"])
                self.A("act", (lambda bo=bo, bcol=bcol: lambda e: e.activation(
                    out=sq16, in_=self.ps[bo][:, 0:T], func=AF.Square, bias=bcol))(),
                    reads=[("ps", bo), "cols"], writes=["sq16"])
                if t == NT - 1 and c == 3:
                    self.issue_next_fill(part="in", extra_writes=[("dgr", cc, tp) for cc in range(4) for tp in range(KCONV)])
            stats(DC - 1)

        def LN(t):
            self.A("dve", lambda e: e.tensor_scalar(out=mean, in0=self.ps[6][:, 0:T], scalar1=1.0 / D, scalar2=None, op0=ALU.mult),
                   reads=[("ps", 6)], writes=["mean"])
            self.A("dve", lambda e: e.tensor_tensor(out=var, in0=mean, in1=mean, op=ALU.mult),
                   reads=["mean"], writes=["var"])
            self.A("dve", lambda e: e.scalar_tensor_tensor(out=var, in0=self.ps[7][:, 0:T], scalar=1.0 / D, in1=var,
                                                           op0=ALU.mult, op1=ALU.subtract),
                   reads=[("ps", 7), "var"], writes=["var"])
            self.A("act", lambda e: e.activation(out=rsd, in_=var, func=AF.Sqrt, bias=self.epsc[:, 1:2], scale=1.0),
                   reads=["var", "epsc"], writes=["rsd"])
            self.A("dve", lambda e: e.reciprocal(out=rsd, in_=rsd), reads=["rsd"], writes=["rsd"])

        def LN_main(t):
            for c0 in range(0, DC, 2):
                for c in (c0, c0 + 1):
                    self.A("dve", (lambda c=c: lambda e: e.tensor_tensor(out=co(c), in0=co(c), in1=mean, op=ALU.subtract))(),
                           reads=[("co", c), "mean"], writes=[("co", c)])
                for c in (c0, c0 + 1):
                    self.A("dve", (lambda c=c: lambda e: e.tensor_tensor(out=co(c), in0=co(c), in1=rsd, op=ALU.mult))(),
                           reads=[("co", c), "rsd"], writes=[("co", c)])
                for c in (c0, c0 + 1):
                    self.A("act", (lambda c=c: lambda e: e.activation(
                        out=zv(c, t), in_=co(c), func=AF.Silu, bias=self.col("ln_b", j * DC + c), scale=self.col("ln_g", j * DC + c)))(),
                        reads=[("co", c), "cols"], writes=[("R", c, t)])

        def PW2(t):
            t0, t1 = t * T, (t + 1) * T
            for mc in range(DC):
                bo = self.rotate("p2b", 4)
                for kc in range(DC):
                    rk = [("R", kc, t), ("w2s", kc), "arenaA", "arenaB"] + self.tkeys("xn", kc, t0 - 16, t1 - 16)
                    if 2 <= kc < 4:
                        rk.append(WB)
                    S.add("pe", (lambda kc=kc, mc=mc, bo=bo: lambda e: e.matmul(
                        self.ps[bo][:, 0:T], lhsT=w2v(kc, mc), rhs=zv(kc, t),
                        start=(kc == 0), stop=(kc == DC - 1)))(),
                        reads=rk, writes=[("ps", bo)])
                S.add("dve", (lambda mc=mc, bo=bo: lambda e: e.scalar_tensor_tensor(
                    out=self.xv(mc, t0, t1), in0=self.ps[bo][:, 0:T], scalar=self.col("b_pw2", j * DC + mc), in1=self.xv(mc, t0, t1),
                    op0=ALU.add, op1=ALU.add))(),
                    reads=[("ps", bo), ("x", mc, t), "cols"], writes=[("x", mc, t)])
                self.mask_fix(lambda a_, b_, mc=mc: self.xv(mc, a_, b_), t, [("x", mc, t)])

        for t in range(NT):
            B1(t)
            LN(t)
            if t > 0:
                PW2(t - 1)
            LN_main(t)
        PW2(NT - 1)
        self.issue_next_fill(part="out", extra_writes=[("w2s", 0), ("w2s", 1)])
        self.after_conv = True

    def final_norm(self, p, raw=False):
        S = self.S
        self.phase_begin("BN" if self.after_conv else "B")
        NB = (NORM_OFF - 3456) // T
        ovb = lambda k: self.ar32(3456 + k * T, T)

        def tile(t, rstd, rk):
            t0, t1 = t * T, (t + 1) * T
            for c in range(DC):
                k = self.rotate("ov", NB)
                if raw:
                    self.A("dve", (lambda c=c, k=k: lambda e: e.tensor_copy(out=ovb(k), in_=self.xv(c, t0, t1)))(),
                           reads=[("x", c, t)], writes=[("ov", k)])
                else:
                    self.A("dve", (lambda c=c, k=k: lambda e: e.scalar_tensor_tensor(
                        out=ovb(k), in0=self.xv(c, t0, t1), scalar=self.g32col(DEPTH * 3 * DC + c), in1=rstd,
                        op0=ALU.mult, op1=ALU.mult))(),
                        reads=[("x", c, t), rk, "g32f"], writes=[("ov", k)], region="BN")
                self.A("sp", (lambda c=c, k=k: lambda e: e.dma_start(out=self.yT[p, c * 128:(c + 1) * 128, t0:t1], in_=ovb(k)))(),
                       reads=[("ov", k)], writes=[("y", p, t, c)], chan=self.ch_y)

        nxt = (None, None) if raw else self.rms_rstd(0, T)
        for t in range(NT):
            cur = nxt
            if not raw and t + 1 < NT:
                nxt = self.rms_rstd((t + 1) * T, (t + 2) * T)
            tile(t, *cur)


def make_cols(norm_g, final_g, pool_b, pool_scale, conv_b_pw1, conv_b_dw, conv_ln_g, conv_ln_b, conv_b_pw2, conv_w_dw):
    cols = np.zeros((128, NCOLS), np.float32)

    def put(name, idx0, vec):
        v = np.asarray(vec, np.float32).reshape(-1, 128).T
        o = COLS_OFF[name] + idx0
        cols[:, o:o + v.shape[1]] = v

    for l in range(DEPTH):
        for i in range(3):
            put("ng", (l * 3 + i) * DC, norm_g[l, i])
    put("fg", 0, final_g)
    for j in range(2):
        put("pool_b", j * DC, pool_b[j])
        put("pool_s", j * DC, pool_scale[j])
        put("b_pw1", j * 2 * DC, conv_b_pw1[j])
        put("b_dw", j * DC, conv_b_dw[j])
        put("ln_g", j * DC, conv_ln_g[j])
        put("ln_b", j * DC, conv_ln_b[j])
        put("b_pw2", j * DC, conv_b_pw2[j])
        for k in range(KCONV):
            put("w_dw", j * KCONV * DC + k * DC, conv_w_dw[j, k])
    return cols


def window_geometry():
    geo = []
    pos = np.arange(E)
    for w in range(BATCH * 4):
        b, jw = divmod(w, 4)
        start = jw * VALID - HALO
        tok = start + pos
        inside = (tok >= 0) & (tok < SEQ)
        mask = inside.astype(np.float32)
        cnt = np.ones((4, E), np.float32)
        for g, wd in enumerate(POOL_WINDOWS):
            lo = np.clip(tok - wd // 2, 0, SEQ)
            hi = np.clip(tok - wd // 2 + wd, 0, SEQ)
            c = (hi - lo).astype(np.float32)
            cnt[g] = 1.0 / np.where(inside & (c > 0), c, 1.0)
        geo.append((b, start, inside, mask, cnt))
    return geo


_NC_CACHE = {}


def get_nc(**kw):
    key = tuple(sorted(kw.items()))
    if key not in _NC_CACHE:
        _NC_CACHE[key] = Builder(**kw).build()
    return _NC_CACHE[key]


def make_in_maps(x, norm_g, ffn_w_in, ffn_w_out, pool_w, pool_b, pool_scale, conv_w_pw1, conv_b_pw1, conv_w_dw,
                 conv_b_dw, conv_ln_g, conv_ln_b, conv_w_pw2, conv_b_pw2, final_g, n_pass=NPASS):
    x = np.asarray(x, np.float32)
    cols = make_cols(np.asarray(norm_g), np.asarray(final_g), np.asarray(pool_b), np.asarray(pool_scale),
                     np.asarray(conv_b_pw1), np.asarray(conv_b_dw), np.asarray(conv_ln_g), np.asarray(conv_ln_b),
                     np.asarray(conv_b_pw2), np.asarray(conv_w_dw))
    geo = window_geometry()
    shared = {
        "cols": cols,
        "ident": np.eye(128, dtype=np.float32),
        "ffn_w_in": np.ascontiguousarray(ffn_w_in, dtype=np.float32),
        "ffn_w_out": np.ascontiguousarray(ffn_w_out, dtype=np.float32),
        "pool_w": np.ascontiguousarray(pool_w, dtype=np.float32),
        "conv_w_pw1": np.ascontiguousarray(conv_w_pw1, dtype=np.float32),
        "conv_w_pw2": np.ascontiguousarray(conv_w_pw2, dtype=np.float32),
    }
    in_maps = []
    for core in range(NCORES):
        xT = np.zeros((n_pass, D, E), np.float32)
        mask = np.zeros((n_pass, 128, E), np.float32)
        cnt = np.ones((n_pass, 4, 128, E), np.float32)
        for p in range(n_pass):
            w = core * NPASS + p
            b, start, inside, m, c = geo[w]
            lo = max(start, 0)
            hi = min(start + E, SEQ)
            xT[p][:, lo - start:hi - start] = x[b, lo:hi, :].T
            mask[p] = m[None, :]
            cnt[p] = c[:, None, :]
        m = dict(shared)
        m["xT"] = xT
        m["mask"] = mask
        m["cnt"] = cnt
        in_maps.append(m)
    return in_maps


def assemble(results, n_pass=NPASS):
    out = np.zeros((BATCH, SEQ, D), np.float32)
    for core in range(NCORES):
        yT = results[core]["yT"]
        for p in range(n_pass):
            w = core * NPASS + p
            b, jw = divmod(w, 4)
            out[b, jw * VALID:(jw + 1) * VALID, :] = yT[p][:, HALO:HALO + VALID].T
    return out


def kernel(**inputs):
    nc = get_nc()
    in_maps = make_in_maps(**inputs)
    res = run_bass_kernel_spmd(nc, in_maps, core_ids=list(range(NCORES)))
    return assemble(res.results)
```

```python
import contextlib
import numpy as np
import concourse.bass as bass
import concourse.mybir as mybir
from concourse.bass_utils import run_bass_kernel_spmd

F32 = mybir.dt.float32
BF16 = mybir.dt.bfloat16
ALU = mybir.AluOpType
AF = mybir.ActivationFunctionType

D = 1024
DC = 8
DFF = 2816
FC = 22
DEPTH = 4
KCONV = 31
SEQ = 8192
BATCH = 4
NCORES = 8
NPASS = 2
VALID = 2048
HALO = 56
T = 432
NT = 5
E = T * NT
MARG = 16
EM = E + 2 * MARG
RMS_EPS = 1e-6
LN_EPS = 1e-5
GROUPS = [(0, 6), (6, 12), (12, 17), (17, 22)]
GMAX = 6
WIN_SZ = DC * 2 * GMAX * 128
WSET_SZ = WIN_SZ + GMAX * D
POOL_WINDOWS = (2, 4, 8, 16)
HW = T + 16
DGSZ = KCONV * 128
W2S_OFF = 4 * DGSZ
ARENA = 7280
NORM_OFF = ARENA - 1344

CL, CR = (56, 72), (2088, 2104)
ML, MR = (0, 64), (2096, 2160)
ENGS = ("pe", "act", "dve", "pool", "sp")
SEM_SPAN = 20000


def _cols_layout():
    off = {}
    n = 0

    def add(name, k):
        nonlocal n
        off[name] = n
        n += k

    add("ng", DEPTH * 3 * DC)
    add("fg", DC)
    add("pool_b", 2 * DC)
    add("pool_s", 2 * DC)
    add("b_pw1", 2 * 2 * DC)
    add("b_dw", 2 * DC)
    add("ln_g", 2 * DC)
    add("ln_b", 2 * DC)
    add("b_pw2", 2 * DC)
    add("w_dw", 2 * KCONV * DC)
    return off, n


COLS_OFF, NCOLS = _cols_layout()


class Chan:
    def __init__(self, sem):
        self.sem = sem
        self.count = 0


class Op:
    __slots__ = ("eng", "fn", "deps", "need", "chan", "sigval", "sem", "is_dma", "short")

    def __init__(self, eng, fn, chan):
        self.eng = eng
        self.fn = fn
        self.deps = []
        self.need = False
        self.chan = chan
        self.sigval = None
        self.sem = None
        self.is_dma = chan is not None
        self.short = False


class Sched:
    def __init__(self, same_engine_sync=True):
        self.streams = {e: [] for e in ENGS}
        self.lastw = {}
        self.rd_eng = {}
        self.rd_dma = {}
        self.same_engine_sync = same_engine_sync

    def add(self, eng, fn, reads=(), writes=(), chan=None, short=False):
        op = Op(eng, fn, chan)
        op.short = short
        deps = {}

        def dep(o):
            if o is None or o is op:
                return
            if o.is_dma:
                deps[id(o)] = (o, 16 * o.chan.count)
                return
            if o.eng == eng and not op.is_dma:
                if eng == "pe" or not (self.same_engine_sync or o.short or short):
                    return
            deps[id(o)] = (o, None)

        for k in reads:
            dep(self.lastw.get(k))
        for k in writes:
            dep(self.lastw.get(k))
            for o in self.rd_eng.get(k, {}).values():
                dep(o)
            for o in self.rd_dma.get(k, ()):
                dep(o)
        for k in writes:
            self.lastw[k] = op
            self.rd_eng[k] = {}
            self.rd_dma[k] = []
        for k in reads:
            if op.is_dma:
                self.rd_dma.setdefault(k, []).append(op)
            else:
                self.rd_eng.setdefault(k, {})[eng] = op
        if chan is not None:
            chan.count += 1
            op.sigval = 16 * chan.count
            op.sem = chan.sem
        op.deps = list(deps.values())
        self.streams[eng].append(op)
        return op

    def emit(self, block, eng_sems):
        for e in ENGS:
            for op in self.streams[e]:
                for d, _ in op.deps:
                    d.need = True
        for e in ENGS:
            cnt = 0
            for op in self.streams[e]:
                if op.is_dma:
                    continue
                if op.need:
                    si = cnt // SEM_SPAN
                    cnt += 1
                    op.sem = eng_sems[e][si]
                    op.sigval = cnt - si * SEM_SPAN
        self.stats = {}

        def run_stream(e):
            def body(eng):
                waited = {}
                nw = 0
                for op in self.streams[e]:
                    for d, v in op.deps:
                        val = d.sigval if v is None else v
                        key = id(d.sem)
                        if waited.get(key, 0) >= val:
                            continue
                        waited[key] = val
                        eng.wait_ge(d.sem, val)
                        nw += 1
                    if op.fn is None:
                        continue
                    ins = op.fn(eng)
                    if op.is_dma:
                        ins.then_inc(op.sem, 16)
                    elif op.need:
                        ins.then_inc(op.sem, 1)
                self.stats[e] = (len(self.streams[e]), nw)
            return body

        block.tensor(run_stream("pe"))
        block.scalar(run_stream("act"))
        block.vector(run_stream("dve"))
        block.gpsimd(run_stream("pool"))
        block.sync(run_stream("sp"))


class Builder:
    def __init__(self, n_pass=NPASS, layers=DEPTH, same_engine_sync=True, stop_after=None):
        self.n_pass = n_pass
        self.layers = layers
        self.stop_after = stop_after
        self.S = Sched(same_engine_sync)
        self.nc = bass.Bass("TRN2", target_bir_lowering=False)
        self.rot = {}
        self.fill_list = []
        self.fill_pos = 0
        self.fill_set = {}
        self.wtoggle = 0
        self.pending_fill = None
        self.cur_region = "AB"
        self.after_conv = False

    def dram_in(self, name, shape, dt=F32):
        return self.nc.dram_tensor(name, list(shape), dt, kind="ExternalInput").ap()

    def rotate(self, name, n):
        i = self.rot.get(name, 0)
        self.rot[name] = (i + 1) % n
        return i

    def col(self, name, idx):
        c = COLS_OFF[name] + idx
        return self.cols[:, c:c + 1]

    def build(self):
        nc = self.nc
        npass = self.n_pass
        self.xT = self.dram_in("xT", [npass, D, E])
        self.maskd = self.dram_in("mask", [npass, 128, E])
        self.cntd = self.dram_in("cnt", [npass, 4, 128, E])
        self.colsd = self.dram_in("cols", [128, NCOLS])
        self.identd = self.dram_in("ident", [128, 128])
        self.w_in = self.dram_in("ffn_w_in", [DEPTH, 2, D, 2 * DFF])
        self.w_out = self.dram_in("ffn_w_out", [DEPTH, 2, DFF, D])
        self.pool_w = self.dram_in("pool_w", [2, 4, 256, 256])
        self.w_pw1 = self.dram_in("conv_w_pw1", [2, D, 2 * D])
        self.w_pw2 = self.dram_in("conv_w_pw2", [2, D, D])
        self.yT = nc.dram_tensor("yT", [npass, D, E], F32, kind="ExternalOutput").ap()

        with contextlib.ExitStack() as st:
            def sb(name, shape, dt):
                return st.enter_context(nc.sbuf_tensor(name, list(shape), dt))

            self.x = sb("x_sb", [128, DC * E], F32)
            self.xn = sb("xn_sb", [128, DC * EM], BF16)
            self.wset = [sb("wset0", [128, WSET_SZ], BF16), sb("wset1", [128, WSET_SZ], BF16)]
            self.cols = sb("cols_sb", [128, NCOLS], F32)
            self.g32 = sb("g32_sb", [128, (DEPTH * 3 + 1) * DC + 2 * DC], F32)
            self.bndc = sb("bndc_sb", [128, 2 * 4 * 16], F32)
            self.bndm = sb("bndm_sb", [128, 2 * 64], F32)
            self.ones = sb("ones_sb", [128, 128], BF16)
            self.ident = sb("ident_sb", [128, 128], BF16)
            self.scr = sb("scr_sb", [128, 2], F32)
            self.epsc = sb("eps_sb", [128, 2], F32)
            self.arena = sb("arena", [128, ARENA], F32)
            self.ps = [st.enter_context(nc.psum_tensor(f"ps{i}", [128, 512], F32)) for i in range(8)]
            sems = {e: [st.enter_context(nc.semaphore(f"s_{e}{i}")) for i in range(4)] for e in ENGS}

            def ch(name):
                return Chan(st.enter_context(nc.semaphore(name)))

            self.ch_x = [ch(f"c_x{t}") for t in range(NT)]
            self.ch_w = [ch("c_w0"), ch("c_w1")]
            self.ch_m = [ch("c_m0"), ch("c_m1")]
            self.ch_c = ch("c_c")
            self.ch_y = ch("c_y")
            self.ch_k = ch("c_k")

            self.program()

            with nc.Block() as block:
                self.S.emit(block, sems)
        return nc

    def xv(self, c, a, b):
        return self.x[:, c * E + a: c * E + b]

    def xnv(self, c, a, b):
        return self.xn[:, c * EM + MARG + a: c * EM + MARG + b]

    def ar32(self, off, n):
        return self.arena[:, off:off + n]

    def ar16(self, off32, n16):
        v = self.arena[:, off32:off32 + (n16 + 1) // 2].bitcast(BF16)
        return v[:, 0:n16]

    @staticmethod
    def tkeys(name, c, a, b):
        t0 = max(a, 0) // T
        t1 = (min(b, E) - 1) // T
        return [(name, c, t) for t in range(t0, t1 + 1)]

    def A(self, eng, fn, reads=(), writes=(), chan=None, short=False, region=None):
        region = region or self.cur_region
        return self.S.add(eng, fn, list(reads) + ["arena" + r for r in region], writes, chan, short)

    def mask_fix(self, view_fn, t, keys):
        if t == 0:
            side, (a, b) = 0, ML
        elif t == NT - 1:
            side, (a, b) = 1, MR
        else:
            return
        v = view_fn(a, b)
        m = self.bndm[:, side * 64:(side + 1) * 64]
        self.S.add("dve", lambda e: e.tensor_tensor(out=v, in0=v, in1=m, op=ALU.mult),
                   reads=list(keys) + [("bndm", side)], writes=list(keys), short=True)

    def phase_begin(self, regions="ABN"):
        self.cur_region = regions
        self.S.add("dve", lambda e: e.memset(self.scr[:, 0:1], 0.0), writes=["arena" + r for r in regions], short=True)

    def plan_fills(self):
        fl = []
        for p in range(self.n_pass):
            for l in range(self.layers):
                for g in range(len(GROUPS)):
                    fl.append(("ffn", p, l, 0, g))
                if self.stop_after == (l, 0):
                    break
                if l % 2 == 0:
                    fl.append(("poolw", p, l))
                else:
                    fl.append(("w1", p, l))
                    fl.append(("w2", p, l))
                if self.stop_after == (l, 1):
                    break
                for g in range(len(GROUPS)):
                    fl.append(("ffn", p, l, 1, g))
                if self.stop_after == (l, 2):
                    break
        self.fill_list = fl

    def issue_next_fill(self, part=None, extra_writes=()):
        if part == "out":
            if self.pending_fill is None:
                return
            f, s = self.pending_fill
            self.pending_fill = None
        else:
            if self.fill_pos >= len(self.fill_list):
                return
            f = self.fill_list[self.fill_pos]
            self.fill_pos += 1
            s = self.wtoggle
            self.wtoggle ^= 1
            self.fill_set[f] = s
            if part == "in":
                assert f[0] == "ffn"
                self.pending_fill = (f, s)
        S = self.S
        ws = self.wset[s]
        chn = self.ch_w[s]
        kind = f[0]
        first = [part != "out"]

        xw = [list(extra_writes)]

        def dma(dst, src):
            wr = ([("wset", s)] if first[0] else []) + xw[0]
            first[0] = False
            xw[0] = []
            S.add("pool", lambda e: e.dma_start(out=dst, in_=src), writes=wr, chan=chn)

        if kind == "ffn":
            _, p, l, i, g = f
            f0, f1 = GROUPS[g]
            n = f1 - f0
            if part != "out":
                dstw = ws[:, 0:WIN_SZ].rearrange("p (kc h j) -> p kc h j", kc=DC, h=2, j=GMAX * 128)
                for half in range(2):
                    src = self.w_in[l, i, :, half * DFF + f0 * 128: half * DFF + f1 * 128].rearrange("(kc p) f -> p kc f", p=128)
                    dst = dstw[:, :, half, 0:n * 128]
                    dma(dst, src)
            if part != "in":
                src = self.w_out[l, i, f0 * 128:f1 * 128, :].rearrange("(n p) d -> p n d", p=128)
                dst = ws[:, WIN_SZ:WIN_SZ + n * D].rearrange("p (n d) -> p n d", d=D)
                dma(dst, src)
        elif kind == "poolw":
            _, p, l = f
            j = l // 2
            src = self.pool_w[j].rearrange("g (kc p) m -> p g kc m", p=128)
            dst = ws[:, 10752:10752 + 2048].rearrange("p (g kc m) -> p g kc m", g=4, kc=2)
            dma(dst, src)
        elif kind == "w1":
            _, p, l = f
            j = l // 2
            src = self.w_pw1[j].rearrange("(kc p) m -> p kc m", p=128)
            dst = ws[:, 0:DC * 2 * D].rearrange("p (kc m) -> p kc m", kc=DC)
            dma(dst, src)
        elif kind == "w2":
            _, p, l = f
            j = l // 2
            src = self.w_pw2[j][2 * 128:4 * 128, :].rearrange("(kc p) m -> p kc m", p=128)
            dst = ws[:, W2S_OFF:W2S_OFF + 2 * D].rearrange("p (kc m) -> p kc m", kc=2)
            dma(dst, src)

    def program(self):
        S = self.S
        self.plan_fills()
        S.add("sp", lambda e: e.dma_start(out=self.cols[:], in_=self.colsd[:, :]), writes=["cols"], chan=self.ch_k)
        self.issue_next_fill()
        S.add("dve", lambda e: e.memset(self.ones[:], 1.0), writes=["ones"], short=True)
        S.add("dve", lambda e: e.memset(self.epsc[:, 0:1], float(RMS_EPS)), writes=["epsc"], short=True)
        S.add("dve", lambda e: e.memset(self.epsc[:, 1:2], float(LN_EPS)), reads=["epsc"], writes=["epsc"], short=True)
        S.add("dve", lambda e: e.memset(self.xn[:], 0.0),
              writes=[("xn", c, t) for c in range(DC) for t in range(NT)])
        ng0 = COLS_OFF["ng"]
        S.add("dve", lambda e: e.tensor_scalar(out=self.g32[:, 0:DEPTH * 3 * DC], in0=self.cols[:, ng0:ng0 + DEPTH * 3 * DC],
                                               scalar1=1.0, scalar2=None, op0=ALU.mult), reads=["cols"], writes=["g32"], short=True)
        fg0 = COLS_OFF["fg"]
        S.add("dve", lambda e: e.tensor_scalar(out=self.g32[:, DEPTH * 3 * DC:(DEPTH * 3 + 1) * DC], in0=self.cols[:, fg0:fg0 + DC],
                                               scalar1=1.0, scalar2=None, op0=ALU.mult), reads=["cols"], writes=["g32f"], short=True)
        S.add("pool", lambda e: e.dma_start(out=self.ident[:], in_=self.identd[:, :]), writes=["ident"], chan=self.ch_m[0])
        BS0 = (DEPTH * 3 + 1) * DC
        pb0, ps0 = COLS_OFF["pool_b"], COLS_OFF["pool_s"]
        S.add("dve", lambda e: e.tensor_tensor(out=self.g32[:, BS0:BS0 + 2 * DC], in0=self.cols[:, pb0:pb0 + 2 * DC],
                                               in1=self.cols[:, ps0:ps0 + 2 * DC], op=ALU.mult), reads=["cols"], writes=["bs"], short=True)

        for p in range(self.n_pass):
            self.p = p
            for side, (ca_, cb_) in enumerate((CL, CR)):
                dst = self.bndc[:, side * 64:(side + 1) * 64].rearrange("p (g n) -> p g n", g=4)
                S.add("sp", (lambda dst=dst, ca_=ca_, cb_=cb_, p=p: lambda e: e.dma_start(
                    out=dst, in_=self.cntd[p, :, :, ca_:cb_].rearrange("g p n -> p g n")))(),
                    writes=[("bndc", side)], chan=self.ch_c)
            for side, (ma_, mb_) in enumerate((ML, MR)):
                S.add("sp", (lambda side=side, ma_=ma_, mb_=mb_, p=p: lambda e: e.dma_start(
                    out=self.bndm[:, side * 64:(side + 1) * 64], in_=self.maskd[p, :, ma_:mb_]))(),
                    writes=[("bndm", side)], chan=self.ch_c)
            for t in range(NT):
                for c in range(DC):
                    S.add("sp", (lambda c=c, p=p, t=t: lambda e: e.dma_start(
                        out=self.xv(c, t * T, (t + 1) * T), in_=self.xT[p, c * 128:(c + 1) * 128, t * T:(t + 1) * T]))(),
                        writes=[("x", c, t)], chan=self.ch_x[t])
            done = False
            for l in range(self.layers):
                self.ffn(p, l, 0)
                if self.stop_after == (l, 0):
                    done = True
                    break
                hook = None
                if l % 2 == 0:
                    hook = self.pool_mixer(p, l, defer=False)
                else:
                    self.conv_mixer(p, l)
                if self.stop_after == (l, 1):
                    done = True
                    break
                self.ffn(p, l, 1, hook=hook)
                if self.stop_after == (l, 2):
                    done = True
                    break
            self.final_norm(p, raw=done)
        S.add("sp", None, reads=[("y", p, t, c) for p in range(self.n_pass) for t in range(NT) for c in range(DC)])

    def rms_rstd(self, a, b):
        n = b - a
        i = self.rotate("rstd", 2)
        rstd = self.ar32(NORM_OFF + i * 448, n)
        for c in range(DC):
            j = self.rotate("sq", 2)
            sq = self.ar16(NORM_OFF + 896 + j * 224, n)
            self.A("act", (lambda sq=sq, c=c: lambda e: e.activation(out=sq, in_=self.xv(c, a, b), func=AF.Square))(),
                   reads=self.tkeys("x", c, a, b), writes=[("sq", j)], region="N")
            self.A("pe", (lambda sq=sq, c=c: lambda e: e.matmul(self.ps[7][:, 0:n], lhsT=self.ones[:], rhs=sq,
                                                              start=(c == 0), stop=(c == DC - 1)))(),
                   reads=[("sq", j), "ones"], writes=[("ps", 7)], region="N")
        self.A("act", lambda e: e.activation(out=rstd, in_=self.ps[7][:, 0:n], func=AF.Sqrt, bias=self.epsc[:, 0:1], scale=1.0 / D),
               reads=[("ps", 7), "epsc"], writes=[("rstd", i)], region="N")
        self.A("dve", lambda e: e.reciprocal(out=rstd, in_=rstd), reads=[("rstd", i)], writes=[("rstd", i)], region="N")
        return rstd, ("rstd", i)

    def rms_squares8(self, a, b):
        n = b - a
        for c in range(DC):
            sq = self.ar16(3456 + c * 216, n)
            self.A("act", (lambda sq=sq, c=c: lambda e: e.activation(out=sq, in_=self.xv(c, a, b), func=AF.Square))(),
                   reads=self.tkeys("x", c, a, b), writes=[("sqB", c)], region="B")

    def rms_stats8(self, a, b):
        n = b - a
        i = self.rotate("rstd", 2)
        rstd = self.ar32(NORM_OFF + i * 448, n)
        for c in range(DC):
            sq = self.ar16(3456 + c * 216, n)
            self.A("pe", (lambda sq=sq, c=c: lambda e: e.matmul(self.ps[7][:, 0:n], lhsT=self.ones[:], rhs=sq,
                                                              start=(c == 0), stop=(c == DC - 1)))(),
                   reads=[("sqB", c), "ones"], writes=[("ps", 7)], region="BN")
        self.A("act", lambda e: e.activation(out=rstd, in_=self.ps[7][:, 0:n], func=AF.Sqrt, bias=self.epsc[:, 0:1], scale=1.0 / D),
               reads=[("ps", 7), "epsc"], writes=[("rstd", i)], region="N")
        self.A("dve", lambda e: e.reciprocal(out=rstd, in_=rstd), reads=[("rstd", i)], writes=[("rstd", i)], region="N")
        return rstd, ("rstd", i)

    def g32col(self, idx):
        return self.g32[:, idx:idx + 1]

    def ffn_norm_tile(self, gbase, t, t0, t1):
        rstd, rk = self.rms_stats8(t0, t1)
        for c in range(DC):
            self.A("dve", (lambda c=c: lambda e: e.scalar_tensor_tensor(
                out=self.xnv(c, t0, t1), in0=self.xv(c, t0, t1), scalar=self.g32col(gbase + c), in1=rstd,
                op0=ALU.mult, op1=ALU.mult))(),
                reads=[("x", c, t), rk, "g32"], writes=[("xn", c, t)], region="N")

    def ffn(self, p, l, i, hook=None):
        if hook is None:
            self.phase_begin("ABN" if self.after_conv else "AB")
            self.after_conv = False
        gbase = (l * 3 + (0 if i == 0 else 2)) * DC
        steps = [(g, t) for g in range(len(GROUPS)) for t in range(NT)]
        full = (self.layers == DEPTH and self.stop_after is None)
        m = ((46, 38, 23, 15)[l] if i == 0 else (38, 23, 15, 0)[l]) if full else HALO
        lo_trim = (HALO - m) // 4 * 4
        hi_end = min(T, -(-(E - HALO + m - (NT - 1) * T) // 4) * 4)

        def trange(t):
            return t * T + (lo_trim if t == 0 else 0), (t * T + hi_end) if t == NT - 1 else (t + 1) * T

        def GT(gs, fi):
            return self.ar16(gs * (GMAX * T // 2) + fi * (T // 2), T)

        def SG(j):
            return self.ar32(2 * GMAX * T // 2 + j * T, T)

        def do_H(k):
            g, t = steps[k]
            if g == 0 and t + 1 < NT:
                if hook is not None:
                    hook(t + 1)
                    if t + 1 == NT - 1:
                        self.issue_next_fill()
                self.rms_squares8(*trange(t + 1))
            s = self.fill_set[("ffn", p, l, i, g)]
            f0, f1 = GROUPS[g]
            n = f1 - f0
            t0, t1 = trange(t)
            nt = t1 - t0
            gs = k % 2
            win = self.wset[s][:, 0:WIN_SZ].rearrange("p (kc h j) -> p kc h j", kc=DC, h=2, j=GMAX * 128)
            for fi in range(n):
                if fi == 2 and g == 0 and t + 1 < NT:
                    self.ffn_norm_tile(gbase, t + 1, *trange(t + 1))
                r = self.rotate("hb", 2)
                bg, bu = r, 2 + r
                for half, bank in ((0, bg), (1, bu)):
                    for kc in range(DC):
                        self.S.add("pe", (lambda kc=kc, half=half, bank=bank, fi=fi: lambda e: e.matmul(
                            self.ps[bank][:, 0:nt], lhsT=win[:, kc, half, fi * 128:(fi + 1) * 128],
                            rhs=self.xnv(kc, t0, t1), start=(kc == 0), stop=(kc == DC - 1)))(),
                            reads=[("wset", s), ("xn", kc, t)], writes=[("ps", bank)])
                j = self.rotate("sg", 2)
                sg = SG(j)
                self.A("act", (lambda sg=sg, bg=bg: lambda e: e.activation(out=sg[:, 0:nt], in_=self.ps[bg][:, 0:nt], func=AF.Silu))(),
                       reads=[("ps", bg)], writes=[("sg", j)])
                gt = GT(gs, fi)
                self.A("dve", (lambda gt=gt, sg=sg, bu=bu: lambda e: e.tensor_tensor(
                    out=gt[:, 0:nt], in0=self.ps[bu][:, 0:nt], in1=sg[:, 0:nt], op=ALU.mult))(),
                    reads=[("ps", bu), ("sg", j)], writes=[("gt", gs, fi)])

        def do_OUT(k):
            g, t = steps[k]
            s = self.fill_set[("ffn", p, l, i, g)]
            f0, f1 = GROUPS[g]
            n = f1 - f0
            t0, t1 = trange(t)
            nt = t1 - t0
            gs = k % 2
            wout = self.wset[s][:, WIN_SZ:WIN_SZ + GMAX * D].rearrange("p (n d) -> p n d", d=D)
            for dc in range(DC):
                bo = 4 + self.rotate("ob", 3)
                for fi in range(n):
                    self.A("pe", (lambda fi=fi, dc=dc, bo=bo: lambda e: e.matmul(
                        self.ps[bo][:, 0:nt], lhsT=wout[:, fi, dc * 128:(dc + 1) * 128], rhs=GT(gs, fi)[:, 0:nt],
                        start=(fi == 0), stop=(fi == n - 1)))(),
                        reads=[("wset", s), ("gt", gs, fi)], writes=[("ps", bo)])
                self.S.add("dve", (lambda dc=dc, bo=bo: lambda e: e.scalar_tensor_tensor(
                    out=self.xv(dc, t0, t1), in0=self.ps[bo][:, 0:nt], scalar=0.5, in1=self.xv(dc, t0, t1),
                    op0=ALU.mult, op1=ALU.add))(),
                    reads=[("ps", bo), ("x", dc, t)], writes=[("x", dc, t)])

        if hook is None:
            self.issue_next_fill()
        else:
            hook(0)
        self.rms_squares8(*trange(0))
        self.ffn_norm_tile(gbase, 0, *trange(0))
        do_H(0)
        for k in range(len(steps)):
            if k + 1 < len(steps):
                do_H(k + 1)
            do_OUT(k)
            if steps[k][1] == NT - 1 and k + 1 < len(steps):
                self.issue_next_fill()

    def load_mask(self, p, t):
        raise NotImplementedError

    def pool_mixer(self, p, l, defer=True):
        S = self.S
        self.phase_begin("B")
        j = l // 2
        s = self.fill_set[("poolw", p, l)]
        self.issue_next_fill()
        wsf = self.wset[s][:, :].bitcast(F32)
        WS = ("wset", s)

        def hv(c, a, b):
            return wsf[:, c * HW + a: c * HW + b]

        def stmp(k, a, b):
            return wsf[:, 8 * HW + k * HW + a: 8 * HW + k * HW + b]

        pw = self.wset[s][:, 10752:10752 + 2048].rearrange("p (g kc m) -> p g kc m", g=4, kc=2)
        pooled = lambda c: self.wset[s][:, 12800 + c * T: 12800 + (c + 1) * T]
        ytmp = lambda k: self.ar32(3456 + k * T, T)
        tmpf = lambda k: self.ar32(4320 + 16 * k, 16)
        BS0 = (DEPTH * 3 + 1) * DC

        def tile(t, rstd, rk):
            t0, t1 = t * T, (t + 1) * T
            ca, cb = t0, min(t1 + 8, E)
            lo = 8
            n = cb - ca
            for c in range(DC):
                if t == 0:
                    S.add("dve", (lambda c=c: lambda e: e.memset(hv(c, 0, 8), 0.0))(), reads=[WS], writes=[("h", c)], short=True)
                else:
                    S.add("dve", (lambda c=c: lambda e: e.tensor_copy(out=hv(c, 0, 8), in_=hv(c, T, T + 8)))(),
                          reads=[WS, ("h", c)], writes=[("h", c)], short=True)
            for c in range(DC):
                S.add("dve", (lambda c=c: lambda e: e.scalar_tensor_tensor(
                    out=hv(c, lo, lo + n), in0=self.xv(c, ca, cb), scalar=self.g32col((l * 3 + 1) * DC + c), in1=rstd,
                    op0=ALU.mult, op1=ALU.mult))(),
                    reads=self.tkeys("x", c, ca, cb) + [rk, "g32", WS, "arenaB", "arenaN", ("h", c)], writes=[("h", c)])
            if lo + n < HW:
                for c in range(DC):
                    S.add("dve", (lambda c=c: lambda e: e.memset(hv(c, lo + n, HW), 0.0))(), reads=[WS], writes=[("h", c)], short=True)
            lohi = [(1, HW, 1, 0), (2, HW - 1, 1, 1), (4, HW - 3, 2, 2), (8, HW - 7, 4, 4)]
            for g in range(4):
                pair = (2 * g, 2 * g + 1)
                levels = g + 1
                kbs = {pair[0]: 0, pair[1]: 2}
                srcs = {c: (lambda c: (lambda a_, b_: hv(c, a_, b_)))(c) for c in pair}
                srcks = {c: ("h", c) for c in pair}
                for lv in range(levels):
                    ja, jb, dl, dr = lohi[lv]
                    for c in pair:
                        dk = kbs[c] + (lv % 2)
                        dst = stmp(dk, ja, jb)
                        in0 = srcs[c](ja - dl, jb - dl)
                        in1 = srcs[c](ja + dr, jb + dr)
                        S.add("dve", (lambda dst=dst, in0=in0, in1=in1: lambda e: e.tensor_tensor(out=dst, in0=in0, in1=in1, op=ALU.add))(),
                              reads=[srcks[c], WS], writes=[("stmp", dk)])
                        srcs[c] = (lambda dk: (lambda a_, b_: stmp(dk, a_, b_)))(dk)
                        srcks[c] = ("stmp", dk)
                for c in pair:
                    fsrc = srcs[c](8, 8 + T)
                    S.add("dve", (lambda fsrc=fsrc, c=c, g=g: lambda e: e.scalar_tensor_tensor(
                        out=pooled(c), in0=fsrc, scalar=1.0 / POOL_WINDOWS[g], in1=hv(c, 8, 8 + T),
                        op0=ALU.mult, op1=ALU.subtract))(),
                        reads=[srcks[c], ("h", c), WS, "arenaB"], writes=[("pooled", c)])
                if t in (0, NT - 1):
                    side, (wa, wb) = (0, CL) if t == 0 else (1, CR)
                    la, lb = wa - t0, wb - t0
                    icv = self.bndc[:, side * 64 + g * 16: side * 64 + (g + 1) * 16]
                    for k, c in enumerate(pair):
                        fs2 = srcs[c](8 + la, 8 + lb)
                        tf = tmpf(k)
                        S.add("dve", (lambda fs2=fs2, icv=icv, tf=tf: lambda e: e.tensor_tensor(out=tf, in0=fs2, in1=icv, op=ALU.mult))(),
                              reads=[srcks[c], ("bndc", side), WS, "arenaB"], writes=[("tmpf", k)], short=True)
                    for k, c in enumerate(pair):
                        tf = tmpf(k)
                        S.add("dve", (lambda c=c, la=la, lb=lb, tf=tf: lambda e: e.tensor_tensor(
                            out=pooled(c)[:, la:lb], in0=tf, in1=hv(c, 8 + la, 8 + lb), op=ALU.subtract))(),
                            reads=[("tmpf", k), ("h", c), WS, "arenaB", ("pooled", c)], writes=[("pooled", c)], short=True)
            for g in range(4):
                for mc in range(2):
                    c = 2 * g + mc
                    bo = 4 + self.rotate("ob", 3)
                    for kc in range(2):
                        S.add("pe", (lambda g=g, mc=mc, kc=kc, bo=bo: lambda e: e.matmul(
                            self.ps[bo][:, 0:T], lhsT=pw[:, g, kc, mc * 128:(mc + 1) * 128], rhs=pooled(2 * g + kc),
                            start=(kc == 0), stop=(kc == 1)))(),
                            reads=[WS, ("pooled", 2 * g + kc), "arenaB"], writes=[("ps", bo)])
                    yk = self.rotate("ytmp", 2)
                    self.A("act", (lambda c=c, bo=bo, yk=yk: lambda e: e.activation(
                        out=ytmp(yk), in_=self.ps[bo][:, 0:T], func=AF.Identity,
                        bias=self.g32[:, BS0 + j * DC + c: BS0 + j * DC + c + 1], scale=self.col("pool_s", j * DC + c)))(),
                        reads=[("ps", bo), "bs", "cols"], writes=[("ytmp", yk)])
                    self.A("dve", (lambda c=c, yk=yk: lambda e: e.tensor_tensor(
                        out=self.xv(c, t0, t1), in0=self.xv(c, t0, t1), in1=ytmp(yk), op=ALU.add))(),
                        reads=[("ytmp", yk), ("x", c, t)], writes=[("x", c, t)])
                    self.mask_fix(lambda a_, b_, c=c: self.xv(c, a_, b_), t, [("x", c, t)])

        rms = lambda t: self.rms_rstd(t * T, min((t + 1) * T + 8, E))
        nxt = rms(0)
        for t in range(NT):
            cur = nxt
            if t + 1 < NT:
                nxt = rms(t + 1)
            tile(t, *cur)
        return None

    def conv_mixer(self, p, l):
        S = self.S
        j = l // 2
        self.phase_begin("AB")
        sA = self.fill_set[("w1", p, l)]
        self.issue_next_fill()
        for c in range(DC):
            S.add("dve", (lambda c=c: lambda e: e.memset(self.xnv(c, -MARG, 0), 0.0))(), writes=[("R", c, 0)], short=True)
        WA = ("wset", sA)
        w1 = self.wset[sA][:, 0:DC * 2 * D].rearrange("p (kc m) -> p kc m", kc=DC)
        xnt = lambda k, c: self.ar16(k * (DC * T // 2) + c * (T // 2), T)
        sgm = lambda k: self.ar32(3456 + k * T, T)
        glv = lambda k: self.ar32(4320 + k * T, T)
        maskA = lambda k: self.ar16(5184 + k * (T // 2), T)
        def normA(t):
            t0, t1 = t * T, (t + 1) * T
            mk = 0
            rstd, rk = self.rms_rstd(t0, t1)
            xk = self.rotate("xnt", 2)
            for c in range(DC):
                self.A("dve", (lambda c=c, xk=xk: lambda e: e.scalar_tensor_tensor(
                    out=xnt(xk, c), in0=self.xv(c, t0, t1), scalar=self.g32col((l * 3 + 1) * DC + c), in1=rstd,
                    op0=ALU.mult, op1=ALU.mult))(),
                    reads=[("x", c, t), rk, "g32"], writes=[("xnt", xk, c)], region="ABN")
            return xk, mk

        def tileA(t, xk, mk):
            t0, t1 = t * T, (t + 1) * T
            for mc in range(DC):
                r = self.rotate("hb", 2)
                ba, bg = r, 2 + r
                for half, bank in ((0, ba), (1, bg)):
                    for kc in range(DC):
                        self.A("pe", (lambda kc=kc, half=half, bank=bank, mc=mc, xk=xk: lambda e: e.matmul(
                            self.ps[bank][:, 0:T], lhsT=w1[:, kc, half * D + mc * 128: half * D + (mc + 1) * 128],
                            rhs=xnt(xk, kc), start=(kc == 0), stop=(kc == DC - 1)))(),
                            reads=[WA, ("xnt", xk, kc)], writes=[("ps", bank)])
                sk = self.rotate("sgm", 2)
                self.A("act", (lambda sk=sk, bg=bg, mc=mc: lambda e: e.activation(
                    out=sgm(sk), in_=self.ps[bg][:, 0:T], func=AF.Sigmoid, bias=self.col("b_pw1", j * 2 * DC + DC + mc)))(),
                    reads=[("ps", bg), "cols"], writes=[("sgm", sk)])
                self.A("dve", (lambda sk=sk, ba=ba, mc=mc: lambda e: e.scalar_tensor_tensor(
                    out=self.xnv(mc, t0, t1), in0=self.ps[ba][:, 0:T], scalar=self.col("b_pw1", j * 2 * DC + mc), in1=sgm(sk),
                    op0=ALU.add, op1=ALU.mult))(),
                    reads=[("ps", ba), ("sgm", sk), "cols"], writes=[("xn", mc, t)])
                self.mask_fix(lambda a_, b_, mc=mc: self.xnv(mc, a_, b_), t, [("xn", mc, t)])

        sB = self.fill_set[("w2", p, l)]
        WB = ("wset", sB)
        dgv = lambda c, tap: (self.wset[sA] if c < 4 else self.wset[sB])[:, (c % 4) * DGSZ + tap * 128:(c % 4) * DGSZ + (tap + 1) * 128]

        def build_diags(c, first_in_set=False):
            for tap in range(KCONV):
                wr = [("dgr", c, tap)]
                rd = ["ident", "cols"]
                if c < 4:
                    if first_in_set and tap == 0:
                        wr.append(WA)
                    else:
                        rd.append(WA)
                else:
                    rd.append(WB)
                S.add("dve", (lambda tap=tap: lambda e: e.tensor_scalar(
                    out=dgv(c, tap), in0=self.ident[:], scalar1=self.col("w_dw", j * KCONV * DC + tap * DC + c), scalar2=None,
                    op0=ALU.mult))(), reads=rd, writes=wr)

        nxt = normA(0)
        for t in range(NT):
            cur = nxt
            if t + 1 < NT:
                nxt = normA(t + 1)
            tileA(t, *cur)
            if t < 4:
                build_diags(4 + t)

        self.phase_begin()
        co = lambda c: self.ar32(c * T, T)
        co16 = self.ar16(3456, T)
        sq16 = self.ar16(3456 + T // 2, T)
        w2a = lambda kc: self.ar16(3888 + (kc - 4) * (D // 2), D)
        mean = self.ar32(NORM_OFF, T)
        var = self.ar32(NORM_OFF + T, T)
        rsd = self.ar32(NORM_OFF + 2 * T, T)
        zv = lambda c, t: self.xnv(c, t * T - 16, (t + 1) * T - 16)

        def w2v(kc, mc):
            if kc < 2:
                return self.wset[sA][:, W2S_OFF + kc * D + mc * 128: W2S_OFF + kc * D + (mc + 1) * 128]
            if kc < 4:
                return self.wset[sB][:, W2S_OFF + (kc - 2) * D + mc * 128: W2S_OFF + (kc - 2) * D + (mc + 1) * 128]
            return w2a(kc)[:, mc * 128:(mc + 1) * 128]

        build_diags(0, first_in_set=True)
        srcA = self.w_pw2[j][0:2 * 128, :].rearrange("(kc p) m -> p kc m", p=128)
        dstA = self.wset[sA][:, W2S_OFF:W2S_OFF + 2 * D].rearrange("p (kc m) -> p kc m", kc=2)
        S.add("pool", lambda e: e.dma_start(out=dstA, in_=srcA), reads=[WA], writes=[("w2s", 0), ("w2s", 1)], chan=self.ch_w[sA])
        for kc in range(4, DC):
            self.A("pool", (lambda kc=kc: lambda e: e.dma_start(out=w2a(kc), in_=self.w_pw2[j][kc * 128:(kc + 1) * 128, :]))(),
                   writes=[("w2s", kc)], chan=self.ch_m[1])
        for c in range(1, 4):
            build_diags(c)

        def B1(t):
            t0, t1 = t * T, (t + 1) * T

            def stats(c):
                self.A("pe", (lambda c=c: lambda e: e.matmul(self.ps[6][:, 0:T], lhsT=self.ones[:], rhs=co16,
                                                            start=(c == 0), stop=(c == DC - 1)))(),
                       reads=["co16", "ones"], writes=[("ps", 6)])
                self.A("pe", (lambda c=c: lambda e: e.matmul(self.ps[7][:, 0:T], lhsT=self.ones[:], rhs=sq16,
                                                            start=(c == 0), stop=(c == DC - 1)))(),
                       reads=["sq16", "ones"], writes=[("ps", 7)])

            for c in range(DC):
                bo = 4 + self.rotate("cvb", 2)
                rkeys = [("R", c, t), ("R", c, t + 1)] + self.tkeys("xn", c, t0 - 15, t1 + 15)
                if c >= 4:
                    rkeys.append(WB)
                for tap in range(KCONV):
                    S.add("pe", (lambda tap=tap, c=c, bo=bo: lambda e: e.matmul(
                        self.ps[bo][:, 0:T], lhsT=dgv(c, tap), rhs=self.xnv(c, t0 + tap - 15, t1 + tap - 15),
                        start=(tap == 0), stop=(tap == KCONV - 1)))(),
                        reads=rkeys + [("dgr", c, tap)], writes=[("ps", bo)])
                if c > 0:
                    stats(c - 1)
                bcol = self.col("b_dw", j * DC + c)
                self.A("act", (lambda c=c, bo=bo, bcol=bcol: lambda e: e.activation(
                    out=co(c), in_=self.ps[bo][:, 0:T], func=AF.Identity, bias=bcol))(),
                    reads=[("ps", bo), "cols"], writes=[("co", c)])
                self.A("act", (lambda bo=bo, bcol=bcol: lambda e: e.activation(
                    out=co16, in_=self.ps[bo][:, 0:T], func=AF.Identity, bias=bcol))(),
                    reads=[("ps", bo), "cols"], writes=["co16"])
                self.A("act", (lambda bo=bo, bcol=bcol: lambda e: e.activation(
                    out=sq16, in_=self.ps[bo][:, 0:T], func=AF.Square, bias=bcol))(),
                    reads=[("ps", bo), "cols"], writes=["sq16"])
                if t == NT - 1 and c == 3:
                    self.issue_next_fill(part="in", extra_writes=[("dgr", cc, tp) for cc in range(4) for tp in range(KCONV)])
            stats(DC - 1)

        def LN(t):
            self.A("dve", lambda e: e.tensor_scalar(out=mean, in0=self.ps[6][:, 0:T], scalar1=1.0 / D, scalar2=None, op0=ALU.mult),
                   reads=[("ps", 6)], writes=["mean"])
            self.A("dve", lambda e: e.tensor_tensor(out=var, in0=mean, in1=mean, op=ALU.mult),
                   reads=["mean"], writes=["var"])
            self.A("dve", lambda e: e.scalar_tensor_tensor(out=var, in0=self.ps[7][:, 0:T], scalar=1.0 / D, in1=var,
                                                           op0=ALU.mult, op1=ALU.subtract),
                   reads=[("ps", 7), "var"], writes=["var"])
            self.A("act", lambda e: e.activation(out=rsd, in_=var, func=AF.Sqrt, bias=self.epsc[:, 1:2], scale=1.0),
                   reads=["var", "epsc"], writes=["rsd"])
            self.A("dve", lambda e: e.reciprocal(out=rsd, in_=rsd), reads=["rsd"], writes=["rsd"])

        def LN_main(t):
            for c0 in range(0, DC, 2):
                for c in (c0, c0 + 1):
                    self.A("dve", (lambda c=c: lambda e: e.tensor_tensor(out=co(c), in0=co(c), in1=mean, op=ALU.subtract))(),
                           reads=[("co", c), "mean"], writes=[("co", c)])
                for c in (c0, c0 + 1):
                    self.A("dve", (lambda c=c: lambda e: e.tensor_tensor(out=co(c), in0=co(c), in1=rsd, op=ALU.mult))(),
                           reads=[("co", c), "rsd"], writes=[("co", c)])
                for c in (c0, c0 + 1):
                    self.A("act", (lambda c=c: lambda e: e.activation(
                        out=zv(c, t), in_=co(c), func=AF.Silu, bias=self.col("ln_b", j * DC + c), scale=self.col("ln_g", j * DC + c)))(),
                        reads=[("co", c), "cols"], writes=[("R", c, t)])

        def PW2(t):
            t0, t1 = t * T, (t + 1) * T
            for mc in range(DC):
                bo = self.rotate("p2b", 4)
                for kc in range(DC):
                    rk = [("R", kc, t), ("w2s", kc), "arenaA", "arenaB"] + self.tkeys("xn", kc, t0 - 16, t1 - 16)
                    if 2 <= kc < 4:
                        rk.append(WB)
                    S.add("pe", (lambda kc=kc, mc=mc, bo=bo: lambda e: e.matmul(
                        self.ps[bo][:, 0:T], lhsT=w2v(kc, mc), rhs=zv(kc, t),
                        start=(kc == 0), stop=(kc == DC - 1)))(),
                        reads=rk, writes=[("ps", bo)])
                S.add("dve", (lambda mc=mc, bo=bo: lambda e: e.scalar_tensor_tensor(
                    out=self.xv(mc, t0, t1), in0=self.ps[bo][:, 0:T], scalar=self.col("b_pw2", j * DC + mc), in1=self.xv(mc, t0, t1),
                    op0=ALU.add, op1=ALU.add))(),
                    reads=[("ps", bo), ("x", mc, t), "cols"], writes=[("x", mc, t)])
                self.mask_fix(lambda a_, b_, mc=mc: self.xv(mc, a_, b_), t, [("x", mc, t)])

        for t in range(NT):
            B1(t)
            LN(t)
            if t > 0:
                PW2(t - 1)
            LN_main(t)
        PW2(NT - 1)
        self.issue_next_fill(part="out", extra_writes=[("w2s", 0), ("w2s", 1)])
        self.after_conv = True

    def final_norm(self, p, raw=False):
        S = self.S
        self.phase_begin("BN" if self.after_conv else "B")
        NB = (NORM_OFF - 3456) // T
        ovb = lambda k: self.ar32(3456 + k * T, T)

        def tile(t, rstd, rk):
            t0, t1 = t * T, (t + 1) * T
            for c in range(DC):
                k = self.rotate("ov", NB)
                if raw:
                    self.A("dve", (lambda c=c, k=k: lambda e: e.tensor_copy(out=ovb(k), in_=self.xv(c, t0, t1)))(),
                           reads=[("x", c, t)], writes=[("ov", k)])
                else:
                    self.A("dve", (lambda c=c, k=k: lambda e: e.scalar_tensor_tensor(
                        out=ovb(k), in0=self.xv(c, t0, t1), scalar=self.g32col(DEPTH * 3 * DC + c), in1=rstd,
                        op0=ALU.mult, op1=ALU.mult))(),
                        reads=[("x", c, t), rk, "g32f"], writes=[("ov", k)], region="BN")
                self.A("sp", (lambda c=c, k=k: lambda e: e.dma_start(out=self.yT[p, c * 128:(c + 1) * 128, t0:t1], in_=ovb(k)))(),
                       reads=[("ov", k)], writes=[("y", p, t, c)], chan=self.ch_y)

        nxt = (None, None) if raw else self.rms_rstd(0, T)
        for t in range(NT):
            cur = nxt
            if not raw and t + 1 < NT:
                nxt = self.rms_rstd((t + 1) * T, (t + 2) * T)
            tile(t, *cur)


def make_cols(norm_g, final_g, pool_b, pool_scale, conv_b_pw1, conv_b_dw, conv_ln_g, conv_ln_b, conv_b_pw2, conv_w_dw):
    cols = np.zeros((128, NCOLS), np.float32)

    def put(name, idx0, vec):
        v = np.asarray(vec, np.float32).reshape(-1, 128).T
        o = COLS_OFF[name] + idx0
        cols[:, o:o + v.shape[1]] = v

    for l in range(DEPTH):
        for i in range(3):
            put("ng", (l * 3 + i) * DC, norm_g[l, i])
    put("fg", 0, final_g)
    for j in range(2):
        put("pool_b", j * DC, pool_b[j])
        put("pool_s", j * DC, pool_scale[j])
        put("b_pw1", j * 2 * DC, conv_b_pw1[j])
        put("b_dw", j * DC, conv_b_dw[j])
        put("ln_g", j * DC, conv_ln_g[j])
        put("ln_b", j * DC, conv_ln_b[j])
        put("b_pw2", j * DC, conv_b_pw2[j])
        for k in range(KCONV):
            put("w_dw", j * KCONV * DC + k * DC, conv_w_dw[j, k])
    return cols


def window_geometry():
    geo = []
    pos = np.arange(E)
    for w in range(BATCH * 4):
        b, jw = divmod(w, 4)
        start = jw * VALID - HALO
        tok = start + pos
        inside = (tok >= 0) & (tok < SEQ)
        mask = inside.astype(np.float32)
        cnt = np.ones((4, E), np.float32)
        for g, wd in enumerate(POOL_WINDOWS):
            lo = np.clip(tok - wd // 2, 0, SEQ)
            hi = np.clip(tok - wd // 2 + wd, 0, SEQ)
            c = (hi - lo).astype(np.float32)
            cnt[g] = 1.0 / np.where(inside & (c > 0), c, 1.0)
        geo.append((b, start, inside, mask, cnt))
    return geo


_NC_CACHE = {}


def get_nc(**kw):
    key = tuple(sorted(kw.items()))
    if key not in _NC_CACHE:
        _NC_CACHE[key] = Builder(**kw).build()
    return _NC_CACHE[key]


def make_in_maps(x, norm_g, ffn_w_in, ffn_w_out, pool_w, pool_b, pool_scale, conv_w_pw1, conv_b_pw1, conv_w_dw,
                 conv_b_dw, conv_ln_g, conv_ln_b, conv_w_pw2, conv_b_pw2, final_g, n_pass=NPASS):
    x = np.asarray(x, np.float32)
    cols = make_cols(np.asarray(norm_g), np.asarray(final_g), np.asarray(pool_b), np.asarray(pool_scale),
                     np.asarray(conv_b_pw1), np.asarray(conv_b_dw), np.asarray(conv_ln_g), np.asarray(conv_ln_b),
                     np.asarray(conv_b_pw2), np.asarray(conv_w_dw))
    geo = window_geometry()
    shared = {
        "cols": cols,
        "ident": np.eye(128, dtype=np.float32),
        "ffn_w_in": np.ascontiguousarray(ffn_w_in, dtype=np.float32),
        "ffn_w_out": np.ascontiguousarray(ffn_w_out, dtype=np.float32),
        "pool_w": np.ascontiguousarray(pool_w, dtype=np.float32),
        "conv_w_pw1": np.ascontiguousarray(conv_w_pw1, dtype=np.float32),
        "conv_w_pw2": np.ascontiguousarray(conv_w_pw2, dtype=np.float32),
    }
    in_maps = []
    for core in range(NCORES):
        xT = np.zeros((n_pass, D, E), np.float32)
        mask = np.zeros((n_pass, 128, E), np.float32)
        cnt = np.ones((n_pass, 4, 128, E), np.float32)
        for p in range(n_pass):
            w = core * NPASS + p
            b, start, inside, m, c = geo[w]
            lo = max(start, 0)
            hi = min(start + E, SEQ)
            xT[p][:, lo - start:hi - start] = x[b, lo:hi, :].T
            mask[p] = m[None, :]
            cnt[p] = c[:, None, :]
        m = dict(shared)
        m["xT"] = xT
        m["mask"] = mask
        m["cnt"] = cnt
        in_maps.append(m)
    return in_maps


def assemble(results, n_pass=NPASS):
    out = np.zeros((BATCH, SEQ, D), np.float32)
    for core in range(NCORES):
        yT = results[core]["yT"]
        for p in range(n_pass):
            w = core * NPASS + p
            b, jw = divmod(w, 4)
            out[b, jw * VALID:(jw + 1) * VALID, :] = yT[p][:, HALO:HALO + VALID].T
    return out


def kernel(**inputs):
    nc = get_nc()
    in_maps = make_in_maps(**inputs)
    res = run_bass_kernel_spmd(nc, in_maps, core_ids=list(range(NCORES)))
    return assemble(res.results)
```
